# Optimizing a Trainium2 kernel written in Bass

```python
import math
import jax, jax.numpy as jnp
from jax import lax
import numpy as np

D_MODEL = 1024
BATCH = 1
SEQ = 16384
DEPTH = 4

HEAD_DIM = 64
A_HEADS = 4
A_CONFIGS = ((128, 1), (512, 4), (2048, 16))
B_HEADS = 4
B_KV_HEADS = 2
B_WINDOW = 128
ROPE_THETA = 150000.0
C_HEADS = 8
C_KV_HEADS = 2
CMP_BLOCK = 32
CMP_STRIDE = 16
CMP_HIDDEN = 256
SEL_BLOCK = 64
SEL_TOPK = 16
C_WINDOW = 512
BLOCK = 128
REL_BUCKETS = 32
REL_MAX_DIST = 2048
D_FF = 4 * D_MODEL
MIX_WIDTH = (A_HEADS + B_HEADS + C_HEADS) * HEAD_DIM
DN_ALPHA = (2 * DEPTH) ** 0.25
DN_BETA = (8 * DEPTH) ** -0.25
LN_EPS = 1e-5
NEG = -1e30
FORCE = 1e4

IN_WIDTHS = (
    A_HEADS * HEAD_DIM, A_HEADS * HEAD_DIM, A_HEADS * HEAD_DIM,
    B_HEADS * HEAD_DIM, B_KV_HEADS * HEAD_DIM, B_KV_HEADS * HEAD_DIM,
    C_HEADS * HEAD_DIM,
    C_KV_HEADS * HEAD_DIM, C_KV_HEADS * HEAD_DIM,
    C_KV_HEADS * HEAD_DIM, C_KV_HEADS * HEAD_DIM,
    C_KV_HEADS * HEAD_DIM, C_KV_HEADS * HEAD_DIM,
    C_HEADS * 3,
)
N_IN = sum(IN_WIDTHS)
SPLIT_POINTS = tuple(int(v) for v in np.cumsum(IN_WIDTHS)[:-1])

kernel_name = "hybrid_dilated_swa_nsa_deepnorm_adaln"


def layer_norm(x, g, b):
    xf = x.astype(jnp.float32)
    mu = jnp.mean(xf, axis=-1, keepdims=True)
    var = jnp.mean(jnp.square(xf - mu), axis=-1, keepdims=True)
    return ((xf - mu) * lax.rsqrt(var + LN_EPS) * g + b).astype(x.dtype)


def t5_bucket(dist):
    exact = REL_BUCKETS // 2
    d = jnp.maximum(dist, 0)
    df = jnp.maximum(d, 1).astype(jnp.float32)
    large = exact + (jnp.log(df / exact) / math.log(REL_MAX_DIST / exact)
                     * (REL_BUCKETS - exact)).astype(jnp.int32)
    large = jnp.minimum(large, REL_BUCKETS - 1)
    return jnp.where(d < exact, d, large)


def rope(t, positions):
    half = t.shape[-1] // 2
    freq = ROPE_THETA ** (-jnp.arange(half, dtype=jnp.float32) / half)
    ang = positions[..., None].astype(jnp.float32) * freq
    cos, sin = jnp.cos(ang)[:, :, None, :], jnp.sin(ang)[:, :, None, :]
    t1, t2 = t[..., :half].astype(jnp.float32), t[..., half:].astype(jnp.float32)
    return jnp.concatenate([t1 * cos - t2 * sin, t2 * cos + t1 * sin], -1).astype(t.dtype)


def banded_attention(q, k, v, max_dist, rel_bias=None, dist_scale=1, sinks=None):
    bsz, L, H, hd = q.shape
    hkv = k.shape[2]
    g = H // hkv
    nblk = L // BLOCK
    nprev = -(-max_dist // BLOCK)
    kctx = (nprev + 1) * BLOCK
    qb = q.reshape(bsz, nblk, BLOCK, hkv, g, hd)

    def context(t):
        tp = jnp.pad(t, ((0, 0), (nprev * BLOCK, 0), (0, 0), (0, 0)))
        tb = tp.reshape(bsz, nblk + nprev, BLOCK, hkv, hd)
        return jnp.concatenate([tb[:, i:i + nblk] for i in range(nprev + 1)], axis=2)

    kc, vc = context(k), context(v)
    s = jnp.einsum('bnqhgd,bnkhd->bnhgqk', qb, kc).astype(jnp.float32) * (hd ** -0.5)
    rel = jnp.arange(kctx) - nprev * BLOCK
    dist = jnp.arange(BLOCK)[:, None] - rel[None, :]
    kpos = jnp.arange(nblk)[:, None, None] * BLOCK + rel[None, None, :]
    valid = (dist >= 0) & (dist <= max_dist) & (kpos >= 0)
    if rel_bias is not None:
        bias = rel_bias[t5_bucket(dist * dist_scale)]
        s = s + jnp.transpose(bias, (2, 0, 1)).reshape(hkv, g, BLOCK, kctx).astype(jnp.float32)
    s = jnp.where(valid[None, :, None, None], s, NEG)
    m = jnp.max(s, axis=-1, keepdims=True)
    if sinks is not None:
        sk = sinks.astype(jnp.float32).reshape(hkv, g)[None, None, :, :, None, None]
        m = jnp.maximum(m, sk)
    e = jnp.exp(s - m)
    den = jnp.sum(e, axis=-1, keepdims=True)
    if sinks is not None:
        den = den + jnp.exp(sk - m)
    p = (e / den).astype(v.dtype)
    out = jnp.einsum('bnhgqk,bnkhd->bnqhgd', p, vc).reshape(bsz, L, H, hd)
    lse = (m + jnp.log(den))[..., 0]
    return out, lse.transpose(0, 1, 4, 2, 3).reshape(bsz, L, H)


def dilated_attention(q, k, v, rel_bias):
    bsz, S, H, hd = q.shape
    outs, lses = [], []
    for window, dil in A_CONFIGS:
        span = dil * BLOCK
        L = -(-S // span) * span

        def to_sub(t):
            tp = jnp.pad(t, ((0, 0), (0, L - S), (0, 0), (0, 0)))
            return tp.reshape(bsz, L // dil, dil, H, hd).transpose(0, 2, 1, 3, 4).reshape(bsz * dil, L // dil, H, hd)

        o, lse = banded_attention(to_sub(q), to_sub(k), to_sub(v), window // dil,
                                  rel_bias=rel_bias, dist_scale=dil)
        outs.append(o.reshape(bsz, dil, L // dil, H, hd).transpose(0, 2, 1, 3, 4).reshape(bsz, L, H, hd)[:, :S])
        lses.append(lse.reshape(bsz, dil, L // dil, H).transpose(0, 2, 1, 3).reshape(bsz, L, H)[:, :S])
    w = jax.nn.softmax(jnp.stack(lses, axis=-1), axis=-1)
    return jnp.einsum('bshdc,bshc->bshd', jnp.stack(outs, axis=-1), w.astype(q.dtype))


def nsa_attention(q, k_cmp_tok, v_cmp_tok, k_slc, v_slc, k_win, v_win, gates,
                  cmp_pos, cmp_w1, cmp_w2, rel_bias):
    bsz, S, H, hd = q.shape
    hkv = k_slc.shape[2]
    g = H // hkv
    scale = hd ** -0.5
    r = CMP_BLOCK // CMP_STRIDE
    n_cmp = S // CMP_STRIDE - r + 1

    def compress(t, pos, w1, w2):
        tb = t.reshape(bsz, S // CMP_STRIDE, CMP_STRIDE, hkv, hd)
        blocks = jnp.concatenate([tb[:, i:i + n_cmp] for i in range(r)], axis=2)
        blocks = blocks + pos[None, None, :, None, :]
        flat = blocks.transpose(0, 1, 3, 2, 4).reshape(bsz, n_cmp, hkv, CMP_BLOCK * hd)
        return jax.nn.gelu(flat @ w1) @ w2

    kc = compress(k_cmp_tok, cmp_pos[0], cmp_w1[0], cmp_w2[0])
    vc = compress(v_cmp_tok, cmp_pos[1], cmp_w1[1], cmp_w2[1])

    n_sel = S // SEL_BLOCK
    topk = min(SEL_TOPK, n_sel)
    ks = k_slc.reshape(bsz, n_sel, SEL_BLOCK, hkv, hd).transpose(0, 3, 1, 2, 4)
    vs = v_slc.reshape(bsz, n_sel, SEL_BLOCK, hkv, hd).transpose(0, 3, 1, 2, 4)

    ratio = SEL_BLOCK // CMP_STRIDE
    front = r - 1
    offs = np.arange(-front, ratio)
    ov_w = jnp.asarray(np.array([max(0, min(SEL_BLOCK, o * CMP_STRIDE + CMP_BLOCK) - max(0, o * CMP_STRIDE))
                                 / CMP_BLOCK for o in offs], dtype=np.float32))
    map_idx = np.arange(n_sel)[:, None] * ratio + offs[None, :] + front
    pad_end = max(0, int(map_idx.max()) + 1 - (n_cmp + front))

    bias_c = rel_bias.reshape(REL_BUCKETS, hkv, g)
    cmp_end = jnp.arange(n_cmp) * CMP_STRIDE + CMP_BLOCK - 1
    sel_start = jnp.arange(n_sel) * SEL_BLOCK
    jj = jnp.arange(n_sel)
    bi = jnp.arange(bsz)[:, None, None, None]
    hi = jnp.arange(hkv)[None, :, None, None]
    hi5 = jnp.arange(hkv)[None, :, None, None, None]
    nblk = S // BLOCK
    qb = q.reshape(bsz, nblk, BLOCK, hkv, g, hd).transpose(1, 0, 2, 3, 4, 5)

    def block_fn(args):
        qn, n = args
        qpos = n * BLOCK + jnp.arange(BLOCK)
        s = jnp.einsum('bqhgd,bchd->bhgqc', qn, kc).astype(jnp.float32) * scale
        valid = cmp_end[None, :] <= qpos[:, None]
        s = jnp.where(valid, s, NEG)
        m = jnp.max(s, axis=-1, keepdims=True)
        e = jnp.where(valid, jnp.exp(s - m), 0.0)
        p = e / jnp.maximum(jnp.sum(e, axis=-1, keepdims=True), 1e-30)
        o_cmp = jnp.einsum('bhgqc,bchd->bqhgd', p.astype(vc.dtype), vc)
        imp = jnp.pad(jnp.sum(p, axis=2), ((0, 0), (0, 0), (0, 0), (front, pad_end)))
        imp_sel = jnp.einsum('bhqjr,r->bhqj', imp[..., map_idx], ov_w)
        cur = qpos // SEL_BLOCK
        sel_valid = sel_start[None, :] <= qpos[:, None]
        forced = (jj[None, :] == 0) | (jj[None, :] == cur[:, None]) | (jj[None, :] == cur[:, None] - 1)
        score = jnp.where(forced, FORCE, jnp.where(sel_valid, imp_sel, NEG))
        _, idx = lax.top_k(score, topk)
        kg = ks[bi, hi, idx]
        vg = vs[bi, hi, idx]
        s2 = jnp.einsum('bqhgd,bhqktd->bhgqkt', qn, kg).astype(jnp.float32) * scale
        kpos = idx[..., None] * SEL_BLOCK + jnp.arange(SEL_BLOCK)
        dist = qpos[None, None, :, None, None] - kpos
        bias = jnp.moveaxis(bias_c[t5_bucket(dist), hi5], -1, 2).astype(jnp.float32)
        s2 = jnp.where((dist >= 0)[:, :, None], s2 + bias, NEG)
        p2 = jax.nn.softmax(s2.reshape(s2.shape[:4] + (topk * SEL_BLOCK,)), axis=-1).reshape(s2.shape)
        o_slc = jnp.einsum('bhgqkt,bhqktd->bqhgd', p2.astype(vg.dtype), vg)
        return o_cmp, o_slc

    o_cmp, o_slc = lax.map(block_fn, (qb, jnp.arange(nblk)))
    o_cmp = o_cmp.transpose(1, 0, 2, 3, 4, 5).reshape(bsz, S, H, hd)
    o_slc = o_slc.transpose(1, 0, 2, 3, 4, 5).reshape(bsz, S, H, hd)
    o_win, _ = banded_attention(q, k_win, v_win, C_WINDOW - 1, rel_bias=rel_bias)
    gt = jax.nn.sigmoid(gates.astype(jnp.float32)).astype(q.dtype)
    return gt[..., 0:1] * o_cmp + gt[..., 1:2] * o_slc + gt[..., 2:3] * o_win


def setup_inputs(seed: int = 0) -> dict:
    key = jax.random.key(seed)
    ks = jax.random.split(key, 16)
    f32 = jnp.float32
    nrm = lambda k, shape, s: jax.random.normal(k, shape, f32) * s
    return {
        "x": nrm(ks[0], (BATCH, SEQ, D_MODEL), 1.0),
        "c": nrm(ks[1], (BATCH, D_MODEL), 1.0),
        "positions": jnp.broadcast_to(jnp.arange(SEQ, dtype=jnp.int32), (BATCH, SEQ)),
        "w_in": nrm(ks[2], (DEPTH, D_MODEL, N_IN), D_MODEL ** -0.5),
        "w_out": nrm(ks[3], (DEPTH, MIX_WIDTH, D_MODEL), DN_BETA * MIX_WIDTH ** -0.5),
        "rel_bias": nrm(ks[4], (REL_BUCKETS, A_HEADS + C_HEADS), 0.2),
        "sinks": nrm(ks[5], (DEPTH, B_HEADS), 1.0),
        "cmp_pos": nrm(ks[6], (DEPTH, 2, CMP_BLOCK, HEAD_DIM), 0.1),
        "cmp_w1": nrm(ks[7], (DEPTH, 2, CMP_BLOCK * HEAD_DIM, CMP_HIDDEN), (CMP_BLOCK * HEAD_DIM) ** -0.5),
        "cmp_w2": nrm(ks[8], (DEPTH, 2, CMP_HIDDEN, HEAD_DIM), CMP_HIDDEN ** -0.5),
        "mlp_w1": nrm(ks[9], (DEPTH, D_MODEL, D_FF), D_MODEL ** -0.5),
        "mlp_w2": nrm(ks[10], (DEPTH, D_FF, D_MODEL), DN_BETA * D_FF ** -0.5),
        "ada_w": nrm(ks[11], (DEPTH, D_MODEL, 6 * D_MODEL), 0.1 * D_MODEL ** -0.5),
        "ada_b": nrm(ks[12], (DEPTH, 6 * D_MODEL), 0.01),
        "ln_g": 1.0 + nrm(ks[13], (DEPTH, 2, D_MODEL), 0.02),
        "ln_b": nrm(ks[14], (DEPTH, 2, D_MODEL), 0.02),
    }


def reference(x, c, positions, w_in, w_out, rel_bias, sinks, cmp_pos, cmp_w1, cmp_w2,
              mlp_w1, mlp_w2, ada_w, ada_b, ln_g, ln_b):
    bsz, seq = x.shape[0], x.shape[1]
    heads = lambda t: t.reshape(bsz, seq, -1, HEAD_DIM)
    bias_a = rel_bias[:, :A_HEADS]
    bias_c = rel_bias[:, A_HEADS:]
    for l in range(DEPTH):
        mod = jax.nn.silu(c) @ ada_w[l] + ada_b[l]
        sh_a, sc_a, g_a, sh_m, sc_m, g_m = [m_[:, None, :] for m_ in jnp.split(mod, 6, axis=-1)]
        h = x * (1 + sc_a) + sh_a
        parts = jnp.split(h @ w_in[l], SPLIT_POINTS, axis=-1)
        aq, ak, av, bq, bk, bv, cq, ckc, cvc, cks, cvs, ckw, cvw = [heads(p) for p in parts[:-1]]
        cg = parts[-1].reshape(bsz, seq, C_HEADS, 3)
        oa = dilated_attention(aq, ak, av, bias_a)
        ob, _ = banded_attention(rope(bq, positions), rope(bk, positions), bv, B_WINDOW - 1, sinks=sinks[l])
        oc = nsa_attention(cq, ckc, cvc, cks, cvs, ckw, cvw, cg, cmp_pos[l], cmp_w1[l], cmp_w2[l], bias_c)
        mixed = jnp.concatenate([oa.reshape(bsz, seq, -1), ob.reshape(bsz, seq, -1),
                                 oc.reshape(bsz, seq, -1)], axis=-1)
        x = layer_norm(DN_ALPHA * x + (1 + g_a) * (mixed @ w_out[l]), ln_g[l, 0], ln_b[l, 0])
        h = x * (1 + sc_m) + sh_m
        f = jnp.square(jax.nn.relu(h @ mlp_w1[l])) @ mlp_w2[l]
        x = layer_norm(DN_ALPHA * x + (1 + g_m) * f, ln_g[l, 1], ln_b[l, 1])
    return x
```

```python
import numpy as np
import concourse.bass as bass
import concourse.mybir as mybir
from concourse.bass_utils import run_bass_kernel_spmd

F32 = mybir.dt.float32
BF16 = mybir.dt.bfloat16
I32 = mybir.dt.int32
AF = mybir.ActivationFunctionType
ALU = mybir.AluOpType
AX = mybir.AxisListType

NCORES = 8
D = 1024
SEQ = 16384
DEPTH = 4
NBLK = SEQ // 128
LB = NBLK // NCORES
LT = LB * 128
NPADT = 3
RT = LB + NPADT
DN_ALPHA = (2 * DEPTH) ** 0.25
LN_EPS = 1e-5
ROPE_THETA = 150000.0

def _perm_half(lo, nheads):
    idx = []
    for h in range(nheads):
        b = lo + 64 * h
        idx += list(range(b + 32, b + 64)) + list(range(b, b + 32))
    return idx

FM_COLS = []
FM_COLS += list(range(0, 256))
FM_COLS += list(range(768, 1024))
FM_COLS += _perm_half(768, 4)
FM_COLS += list(range(1280, 1792))
FM_COLS += list(range(256, 512))
FM_COLS += list(range(1024, 1152))
FM_COLS += _perm_half(1024, 2)
FM_COLS += list(range(2048, 2176))
FM_COLS += list(range(2304, 2432))
FM_COLS += list(range(1792, 1920))
FM_COLS += list(range(1920, 2048))
NFM = len(FM_COLS) // 128
TM_COLS = list(range(1152, 1280)) + list(range(2432, 2560)) + list(range(512, 768)) + \
    list(range(2176, 2304)) + list(range(2560, 2584))
NTM = len(TM_COLS)
NWIN = NFM * 128 + NTM

NQH = 16
NKH = 14
KH_B, KH_CW, KH_A, KH_CS, KH_CC, KH_CV = 0, 2, 4, 8, 10, 12
NVH = 10
VH_B, VH_CW, VH_A, VH_CS = 0, 2, 4, 8


class Res:
    __slots__ = ("name", "lw", "rd")

    def __init__(self, name):
        self.name = name
        self.lw = None
        self.rd = {}


class Sched:
    def __init__(self, nc, stack, n_dma_slots=20):
        self.nc = nc
        self.eng = {"pe": nc.tensor, "act": nc.scalar, "dve": nc.vector,
                    "pool": nc.gpsimd, "sp": nc.sync}
        self.semh = {}
        self.cnt = {}
        self.seen = {k: {} for k in self.eng}
        for k in self.eng:
            self.semh[k] = stack.enter_context(nc.semaphore("s_" + k))
            self.cnt[k] = 0
        self.dma_slots = {}
        self.dma_next = {}
        for q in ("sp", "pool", "act"):
            n = n_dma_slots if q != "act" else 6
            self.dma_slots[q] = []
            for i in range(n):
                key = ("dma", q, i)
                self.semh[key] = stack.enter_context(nc.semaphore("d_%s%d" % (q, i)))
                self.cnt[key] = 0
                self.dma_slots[q].append(key)
            self.dma_next[q] = 0
        self.ninst = 0

    def _deps(self, engine, reads, writes):
        deps = {}

        def need(ev, kind):
            if ev is None:
                return
            k, v = ev
            if k == engine:
                if engine == "pe":
                    return
                if kind != "raw":
                    return
            if deps.get(k, 0) < v:
                deps[k] = v

        for r in reads:
            need(r.lw, "raw")
        for w in writes:
            need(w.lw, "waw")
            for k, v in w.rd.items():
                need((k, v), "war")
        return deps

    def _wait(self, engine, deps):
        e = self.eng[engine]
        seen = self.seen[engine]
        for k, v in deps.items():
            if seen.get(k, 0) >= v:
                continue
            e.wait_ge(self.semh[k], v)
            seen[k] = v
            self.ninst += 1

    def op(self, engine, fn, reads=(), writes=()):
        deps = self._deps(engine, reads, writes)
        self._wait(engine, deps)
        ins = fn(self.eng[engine])
        self.cnt[engine] += 1
        v = self.cnt[engine]
        ins.then_inc(self.semh[engine], 1)
        self.ninst += 1
        for r in reads:
            r.rd[engine] = v
        for w in writes:
            w.lw = (engine, v)
            w.rd = {}
        return ins

    def dma(self, queue, out, in_, reads=(), writes=(), **kw):
        slots = self.dma_slots[queue]
        key = slots[self.dma_next[queue] % len(slots)]
        self.dma_next[queue] += 1
        deps = self._deps(key, reads, writes)
        if self.cnt[key] > 0:
            deps[key] = max(deps.get(key, 0), self.cnt[key])
        self._wait(queue, deps)
        ins = self.eng[queue].dma_start(out=out, in_=in_, **kw)
        self.cnt[key] += 16
        v = self.cnt[key]
        ins.then_inc(self.semh[key], 16)
        self.ninst += 1
        for r in reads:
            r.rd[key] = v
        for w in writes:
            w.lw = (key, v)
            w.rd = {}
        return ins

    def barrier(self):
        for e in self.eng:
            deps = {k: v for k, v in self.cnt.items() if v > 0 and k != e}
            self._wait(e, deps)

    def wait_all(self, engine, resources):
        deps = {}
        for r in resources:
            if r.lw is not None and deps.get(r.lw[0], 0) < r.lw[1]:
                deps[r.lw[0]] = r.lw[1]
            for k, v in r.rd.items():
                if deps.get(k, 0) < v:
                    deps[k] = v
        self._wait(engine, deps)


class Pool:
    def __init__(self, tiles, name):
        self.tiles = tiles
        self.res = [Res("%s%d" % (name, i)) for i in range(len(tiles))]
        self.i = 0

    def get(self):
        k = self.i % len(self.tiles)
        self.i += 1
        return self.tiles[k], self.res[k]


class Ctx:
    pass


def mk_consts(K):
    nc, S, st = K.nc, K.S, K.st
    K.ident_f = st.enter_context(nc.sbuf_tensor("ident_f", [128, 128], F32))
    K.ident_b = st.enter_context(nc.sbuf_tensor("ident_b", [128, 128], BF16))
    K.ones_f = st.enter_context(nc.sbuf_tensor("ones_f", [128, 128], F32))
    K.r_const = Res("consts")
    S.op("pool", lambda e: e.memset(K.ident_f[:], 0.0), writes=[K.r_const])
    S.op("pool", lambda e: e.memset(K.ones_f[:], 1.0), writes=[K.r_const])
    S.op("pool", lambda e: e.affine_select(out=K.ident_f[:], in_=K.ones_f[:], pattern=[[-1, 128]],
                                           compare_op=ALU.is_equal, fill=0.0, base=0,
                                           channel_multiplier=1), reads=[K.r_const], writes=[K.r_const])
    S.op("dve", lambda e: e.tensor_copy(out=K.ident_b[:], in_=K.ident_f[:]), reads=[K.r_const],
         writes=[K.r_const])


def emit_mod(K, c_d, ada_w_d, ada_b_d, modT, r_modT):
    nc, S = K.nc, K.S
    with (nc.sbuf_tensor("m_c", [8, 128], F32) as c8,
          nc.sbuf_tensor("m_s", [128, 8], F32) as sT,
          nc.sbuf_tensor("m_b", [48, 128], F32) as b48,
          nc.sbuf_tensor("m_bT", [128, 48], F32) as bT,
          nc.sbuf_tensor("m_w0", [128, 8, 768], F32) as w0,
          nc.sbuf_tensor("m_w1", [128, 8, 768], F32) as w1,
          nc.psum_tensor("m_ps", [128, 48], F32) as ps,
          nc.psum_tensor("m_ps2", [128, 64], F32) as ps2):
        r_c, r_b, r_ps, r_ps2, r_s = Res("m_c"), Res("m_b"), Res("m_ps"), Res("m_ps2"), Res("m_s")
        S.dma("sp", c8[:], c_d.rearrange("o (c p) -> (o c) p", p=128), writes=[r_c])
        S.dma("sp", b48[:], ada_b_d.rearrange("o (c p) -> (o c) p", p=128), writes=[r_b])
        S.op("pe", lambda e: e.matmul(ps2[:, 0:8], lhsT=c8[:], rhs=K.ident_f[0:8, 0:8], start=True, stop=True),
             reads=[r_c, K.r_const], writes=[r_ps2])
        S.op("pe", lambda e: e.matmul(ps2[:, 8:56], lhsT=b48[:], rhs=K.ident_f[0:48, 0:48], start=True, stop=True),
             reads=[r_b, K.r_const], writes=[r_ps2])
        S.op("act", lambda e: e.activation(out=sT[:], in_=ps2[:, 0:8], func=AF.Silu), reads=[r_ps2], writes=[r_s])
        S.op("dve", lambda e: e.tensor_copy(out=bT[:], in_=ps2[:, 8:56]), reads=[r_ps2], writes=[r_b])
        wp = Pool([w0, w1], "m_w")
        for g in range(8):
            wt, rw = wp.get()
            for kc in range(8):
                S.dma("sp", wt[:, kc, :], ada_w_d[kc * 128:(kc + 1) * 128, g * 768:(g + 1) * 768], writes=[rw])
            for cc in range(6):
                ch = g * 6 + cc
                for kc in range(8):
                    S.op("pe", lambda e, kc=kc, cc=cc, ch=ch, wt=wt: e.matmul(
                        ps[:, ch:ch + 1], lhsT=wt[:, kc, cc * 128:(cc + 1) * 128], rhs=sT[:, kc:kc + 1],
                        start=(kc == 0), stop=(kc == 7)), reads=[rw, r_s], writes=[r_ps])
        S.op("dve", lambda e: e.tensor_tensor(out=modT[:], in0=ps[:], in1=bT[:], op=ALU.add),
             reads=[r_ps, r_b], writes=[r_modT])
        S.barrier()


def load_cast_weight(K, dst_bf, src_d, rows_kc, ncols, r_dst, stage_pool, queue="sp", cast_eng="pool", col_chunk=None):
    S = K.S
    cc = col_chunk or ncols
    for kc in range(rows_kc):
        for c0 in range(0, ncols, cc):
            c1 = min(ncols, c0 + cc)
            stg, rs = stage_pool.get()
            S.dma(queue, stg[:, 0:c1 - c0], src_d[kc * 128:(kc + 1) * 128, c0:c1], writes=[rs])
            S.op(cast_eng, lambda e, stg=stg, kc=kc, c0=c0, c1=c1: e.tensor_copy(
                out=dst_bf[:, kc, c0:c1], in_=stg[:, 0:c1 - c0]), reads=[rs], writes=[r_dst])


def emit_phase1(K, x_d, pos_d, ropec_d, w_in_d, modT, r_modT, qT_d, ktp_d, vp_d, gates_d):
    nc, S = K.nc, K.S
    from contextlib import ExitStack
    with ExitStack() as st2:
        wbf = st2.enter_context(nc.sbuf_tensor("p1_w", [128, 8, NWIN], BF16))
        ws0 = st2.enter_context(nc.sbuf_tensor("p1_ws0", [128, 1024], F32))
        ws1 = st2.enter_context(nc.sbuf_tensor("p1_ws1", [128, 1024], F32))
        ropeC = st2.enter_context(nc.sbuf_tensor("p1_rc", [128, LT], F32))
        ropeS = st2.enter_context(nc.sbuf_tensor("p1_rs", [128, LT], F32))
        posi = st2.enter_context(nc.sbuf_tensor("p1_pi", [128, LT], I32))
        posf = st2.enter_context(nc.sbuf_tensor("p1_pf", [128, LT], F32))
        ang = st2.enter_context(nc.sbuf_tensor("p1_ang", [128, LT], F32))
        ropek = st2.enter_context(nc.sbuf_tensor("p1_rk", [128, 4], F32))
        mod1 = st2.enter_context(nc.sbuf_tensor("p1_m1", [128, 48], F32))
        hT0 = st2.enter_context(nc.sbuf_tensor("p1_h0", [128, 8, 512], BF16))
        hT1 = st2.enter_context(nc.sbuf_tensor("p1_h1", [128, 8, 512], BF16))
        xt0 = st2.enter_context(nc.sbuf_tensor("p1_x0", [128, 1024], F32))
        xt1 = st2.enter_context(nc.sbuf_tensor("p1_x1", [128, 1024], F32))
        xb0 = st2.enter_context(nc.sbuf_tensor("p1_xb0", [128, 1024], BF16))
        xb1 = st2.enter_context(nc.sbuf_tensor("p1_xb1", [128, 1024], BF16))
        fs0 = st2.enter_context(nc.sbuf_tensor("p1_f0", [128, 512], BF16))
        fs1 = st2.enter_context(nc.sbuf_tensor("p1_f1", [128, 512], BF16))
        fs2 = st2.enter_context(nc.sbuf_tensor("p1_f2", [128, 512], BF16))
        tmp0 = st2.enter_context(nc.sbuf_tensor("p1_t0", [128, 512], F32))
        tmp1 = st2.enter_context(nc.sbuf_tensor("p1_t1", [128, 512], F32))
        vs0 = st2.enter_context(nc.sbuf_tensor("p1_v0", [128, NVH, 65], BF16))
        vs1 = st2.enter_context(nc.sbuf_tensor("p1_v1", [128, NVH, 65], BF16))
        gs0 = st2.enter_context(nc.sbuf_tensor("p1_g0", [128, 24], F32))
        gs1 = st2.enter_context(nc.sbuf_tensor("p1_g1", [128, 24], F32))
        zt = st2.enter_context(nc.sbuf_tensor("p1_z", [128, 1024], BF16))
        pt0 = st2.enter_context(nc.psum_tensor("p1_pt0", [128, 1024], BF16))
        pt1 = st2.enter_context(nc.psum_tensor("p1_pt1", [128, 1024], BF16))
        pf0 = st2.enter_context(nc.psum_tensor("p1_pf0", [128, 512], F32))
        pf1 = st2.enter_context(nc.psum_tensor("p1_pf1", [128, 512], F32))
        pf2 = st2.enter_context(nc.psum_tensor("p1_pf2", [128, 512], F32))
        pm0 = st2.enter_context(nc.psum_tensor("p1_pm0", [128, 512], F32))
        pm1 = st2.enter_context(nc.psum_tensor("p1_pm1", [128, 512], F32))
        r_w, r_rope, r_m1, r_z = Res("p1_w"), Res("p1_rope"), Res("p1_m1"), Res("p1_z")
        S.op("pool", lambda e: e.memset(zt[:], 0.0), writes=[r_z])
        for i in range(NPADT):
            for r0 in range(0, NKH * 64, 128):
                S.dma("pool", ktp_d[i, r0:r0 + 128, :], zt[:, 0:128], reads=[r_z])
            S.dma("pool", vp_d[i, :, :], zt[:, 0:NVH * 65], reads=[r_z])
        S.op("dve", lambda e: e.tensor_scalar(out=mod1[:], in0=modT[:], scalar1=1.0, scalar2=None, op0=ALU.add),
             reads=[r_modT], writes=[r_m1])
        load_cast_weight(K, wbf, w_in_d, 8, NWIN, r_w, Pool([ws0, ws1], "p1_ws"), col_chunk=1024)
        S.dma("sp", ropek[:], ropec_d[:, :], writes=[r_rope])
        S.dma("sp", posi[:], bass.AP(tensor=pos_d.tensor, offset=pos_d.offset, ap=[[0, 128], [1, LT]]), writes=[r_rope])
        S.op("dve", lambda e: e.tensor_copy(out=ang[:], in_=posi[:]), reads=[r_rope], writes=[r_rope])
        S.op("dve", lambda e: e.tensor_scalar(out=ang[:], in0=ang[:], scalar1=ropek[:, 0:1], scalar2=None,
                                              op0=ALU.mult), reads=[r_rope], writes=[r_rope])
        MAGIC = 12582912.0
        C1 = 6.28125
        C2 = 2.0 * np.pi - 6.28125

        def reduce_sin(dst, src_add):
            tq = posf
            S.op("dve", lambda e: e.tensor_scalar(out=dst[:], in0=ang[:], scalar1=float(src_add), scalar2=None,
                                                  op0=ALU.add), reads=[r_rope], writes=[r_rope])
            S.op("dve", lambda e: e.tensor_scalar(out=tq[:], in0=dst[:], scalar1=float(1.0 / (2.0 * np.pi)), scalar2=None,
                                                  op0=ALU.mult), reads=[r_rope], writes=[r_rope])
            S.op("dve", lambda e: e.tensor_scalar(out=tq[:], in0=tq[:], scalar1=MAGIC, scalar2=None, op0=ALU.add),
                 reads=[r_rope], writes=[r_rope])
            S.op("dve", lambda e: e.tensor_scalar(out=tq[:], in0=tq[:], scalar1=-MAGIC, scalar2=None, op0=ALU.add),
                 reads=[r_rope], writes=[r_rope])
            S.op("dve", lambda e: e.scalar_tensor_tensor(out=dst[:], in0=tq[:], scalar=-C1, in1=dst[:], op0=ALU.mult,
                                                         op1=ALU.add), reads=[r_rope], writes=[r_rope])
            S.op("dve", lambda e: e.scalar_tensor_tensor(out=dst[:], in0=tq[:], scalar=float(-C2), in1=dst[:], op0=ALU.mult,
                                                         op1=ALU.add), reads=[r_rope], writes=[r_rope])
            S.op("act", lambda e: e.activation(out=dst[:], in_=dst[:], func=AF.Sin), reads=[r_rope], writes=[r_rope])

        reduce_sin(ropeC, np.pi / 2)
        reduce_sin(ropeS, 0.0)
        S.op("dve", lambda e: e.tensor_scalar(out=ropeS[:], in0=ropeS[:], scalar1=ropek[:, 1:2], scalar2=None,
                                              op0=ALU.mult), reads=[r_rope], writes=[r_rope])

        hp = Pool([hT0, hT1], "p1_h")
        xp = Pool([xt0, xt1], "p1_x")
        xbp = Pool([xb0, xb1], "p1_xb")
        ptp = Pool([pt0, pt1], "p1_pt")
        pfp = Pool([pf0, pf1, pf2], "p1_pf")
        fsp = Pool([fs0, fs1, fs2], "p1_fs")
        tmpp = Pool([tmp0, tmp1], "p1_tmp")
        vsp = Pool([vs0, vs1], "p1_vs")
        gsp = Pool([gs0, gs1], "p1_gs")
        r_pm = Res("p1_pm")
        for vs_, rv_ in zip(vsp.tiles, vsp.res):
            S.op("pool", lambda e, vs_=vs_: e.memset(vs_[:], 1.0), writes=[rv_])

        def fm_mm(hT, r_h, g):
            pf, rp = pfp.get()
            for kc in range(8):
                S.op("pe", lambda e, kc=kc, pf=pf: e.matmul(pf[:], lhsT=wbf[:, kc, g * 128:(g + 1) * 128],
                                                            rhs=hT[:, kc, :], start=(kc == 0), stop=(kc == 7)),
                     reads=[r_w, r_h], writes=[rp])
            return pf, rp

        for tg in range(4):
            hT, r_h = hp.get()
            for t in range(4):
                b = tg * 4 + t
                xt, rx = xp.get()
                S.dma("sp", xt[:], x_d[b * 128:(b + 1) * 128, :], writes=[rx])
                xb, rxb = xbp.get()
                S.op("dve", lambda e, xb=xb, xt=xt: e.tensor_copy(out=xb[:], in_=xt[:]), reads=[rx], writes=[rxb])
                pt, rpt = ptp.get()
                for kc in range(8):
                    S.op("pe", lambda e, kc=kc, pt=pt, xb=xb: e.transpose(
                        out=pt[:, kc * 128:(kc + 1) * 128], in_=xb[:, kc * 128:(kc + 1) * 128], identity=K.ident_b[:]),
                        reads=[rxb, K.r_const], writes=[rpt])
                for kc in range(8):
                    S.op("act", lambda e, kc=kc, pt=pt, hT=hT, t=t: e.activation(
                        out=hT[:, kc, t * 128:(t + 1) * 128], in_=pt[:, kc * 128:(kc + 1) * 128], func=AF.Identity,
                        bias=modT[:, kc:kc + 1], scale=mod1[:, 8 + kc:9 + kc]), reads=[rpt, r_m1, r_modT], writes=[r_h])
            tok = slice(tg * 512, (tg + 1) * 512)
            for g in range(NFM):
                if g in (4, 5, 13):
                    continue
                pf, rp = fm_mm(hT, r_h, g)
                fs, rf = fsp.get()
                if g in (2, 3, 12):
                    pr, rpr = fm_mm(hT, r_h, {2: 4, 3: 5, 12: 13}[g])
                    ta, rta = tmpp.get()
                    tb, rtb = tmpp.get()
                    S.op("dve", lambda e, ta=ta, pf=pf: e.tensor_tensor(out=ta[:], in0=pf[:], in1=ropeC[:, tok], op=ALU.mult),
                         reads=[rp, r_rope], writes=[rta])
                    S.op("dve", lambda e, tb=tb, pr=pr: e.tensor_tensor(out=tb[:], in0=pr[:], in1=ropeS[:, tok], op=ALU.mult),
                         reads=[rpr, r_rope], writes=[rtb])
                    S.op("dve", lambda e, ta=ta, tb=tb, fs=fs: e.tensor_tensor(out=fs[:], in0=ta[:], in1=tb[:], op=ALU.add),
                         reads=[rta, rtb], writes=[rf])
                else:
                    S.op("act", lambda e, fs=fs, pf=pf: e.copy(out=fs[:], in_=pf[:]), reads=[rp], writes=[rf])
                src = fs[:].rearrange("p (b t) -> p b t", t=128)
                if g < 10:
                    qg = {0: 0, 1: 1, 2: 2, 3: 3, 6: 4, 7: 5, 8: 6, 9: 7}[g]
                    dst = qT_d[tg * 4:(tg + 1) * 4, qg * 128:(qg + 1) * 128, :].rearrange("b r t -> r b t")
                else:
                    kg = {12: 0, 15: 1, 10: 2, 11: 3, 14: 4, 16: 5, 17: 6}[g]
                    dst = ktp_d[NPADT + tg * 4:NPADT + (tg + 1) * 4, kg * 128:(kg + 1) * 128, :].rearrange("b r t -> r b t")
                S.dma("pool", dst, src, reads=[rf])
            for t in range(4):
                b = tg * 4 + t
                for kc in range(8):
                    S.op("pe", lambda e, kc=kc, t=t: e.matmul(pm0[:], lhsT=hT[:, kc, t * 128:(t + 1) * 128],
                                                            rhs=wbf[:, kc, NFM * 128:NFM * 128 + 512],
                                                            start=(kc == 0), stop=(kc == 7)), reads=[r_w, r_h], writes=[r_pm])
                for kc in range(8):
                    S.op("pe", lambda e, kc=kc, t=t: e.matmul(pm1[:, 0:NTM - 512], lhsT=hT[:, kc, t * 128:(t + 1) * 128],
                                                            rhs=wbf[:, kc, NFM * 128 + 512:NWIN],
                                                            start=(kc == 0), stop=(kc == 7)), reads=[r_w, r_h], writes=[r_pm])
                vs, rv = vsp.get()
                gs, rg = gsp.get()
                S.op("dve", lambda e, vs=vs: e.tensor_copy(out=vs[:, 0:8, 0:64], in_=pm0[:].rearrange("p (h d) -> p h d", d=64)),
                     reads=[r_pm], writes=[rv])
                S.op("dve", lambda e, vs=vs: e.tensor_copy(out=vs[:, 8:10, 0:64],
                                                          in_=pm1[:, 0:128].rearrange("p (h d) -> p h d", d=64)),
                     reads=[r_pm], writes=[rv])
                S.op("dve", lambda e, gs=gs: e.tensor_copy(out=gs[:], in_=pm1[:, 128:152]), reads=[r_pm], writes=[rg])
                S.dma("pool", vp_d[NPADT + b, :, :], vs[:].rearrange("p h d -> p (h d)"), reads=[rv])
                S.dma("pool", gates_d[b * 128:(b + 1) * 128, :], gs[:], reads=[rg])
        S.barrier()


def _tok_index(c):
    return np.concatenate([np.arange((NCORES * j + c) * 128, (NCORES * j + c + 1) * 128) for j in range(LB)])


def _ropec():
    half = 32
    freq = (ROPE_THETA ** (-np.arange(half, dtype=np.float32) / half)).astype(np.float32)
    t = np.zeros((128, 4), np.float32)
    for p in range(128):
        t[p, 0] = freq[p % 32]
        t[p, 1] = -1.0 if (p % 64) < 32 else 1.0
        t[p, 2] = -1.0
    return t


def make_p1_inputs(d, l, x=None):
    x = d["x"][0] if x is None else x
    w_in = np.ascontiguousarray(d["w_in"][l][:, FM_COLS + TM_COLS])
    ins = []
    for c in range(NCORES):
        ti = _tok_index(c)
        ins.append({"c": d["c"], "ada_w": d["ada_w"][l], "ada_b": d["ada_b"][l][None],
                    "x": np.ascontiguousarray(x[ti]), "pos": np.ascontiguousarray(d["positions"][0][ti][None]),
                    "ropec": _ropec(), "w_in": w_in})
    return ins


D_A, D_W, D_S = 18 * 128, 6 * 128, 14 * 128
DTOT = D_A + D_W + D_S
NPADL = 24
NTL = NPADL + NBLK
NTW = NCORES * (LB - 1) + NPADL + 1
NREL = 40
NDYN = 20
FORCE_V = (10000.0, 10001.0, 10002.0)
NEGV = -1.0e30


def _t5_bucket(d):
    d = np.maximum(d, 0)
    df = np.maximum(d, 1).astype(np.float32)
    large = 16 + (np.log(df / np.float32(16)) / np.float32(np.log(2048 / 16)) * np.float32(16)).astype(np.int32)
    large = np.minimum(large, 31)
    return np.where(d < 16, d, large)


def static_tables():
    bm = np.zeros((32, DTOT), np.float32)
    cnt = np.zeros((12, DTOT), np.float32)
    for off, dd, kind in ((0, D_A, "A"), (D_A, D_W, "W"), (D_A + D_W, D_S, "S")):
        dist = np.arange(dd) - 127
        b = _t5_bucket(dist)
        bm[b, off + np.arange(dd)] = 1.0
        if kind == "S":
            bm[31, off:off + dd] -= 1.0
        v = dist >= 0
        if kind == "A":
            c = (v & (dist <= 128)).astype(np.float32) + (v & (dist <= 512) & (dist % 4 == 0)) + \
                (v & (dist <= 2048) & (dist % 16 == 0))
            cnt[0:4, off:off + dd] = c
        elif kind == "W":
            cnt[4:12, off:off + dd] = (v & (dist <= 511))
        else:
            cnt[4:12, off:off + dd] = v
    k = np.arange(128)[:, None]
    q = np.arange(128)[None, :]
    mb = np.zeros((128, 2, 128), np.float32)
    for r in range(2):
        dist = 128 * r + q - k
        mb[:, r, :] = (dist >= 0) & (dist <= 127)
    ovw = {-1: 0.5, 0: 1.0, 1: 1.0, 2: 1.0, 3: 0.5}
    ov = np.zeros((1024, 256), np.float32)
    for jb in range(256):
        for o, w in ovw.items():
            i = 4 * jb + o
            if 0 <= i < 1023:
                ov[i, jb] = w
    ov_t = ov.reshape(8, 128, 256).transpose(1, 0, 2)
    return bm, cnt, mb, ov_t, ov


def percore_tables(c, ov):
    am_abs = np.zeros((LB, 128, 256), np.float32)
    vm_abs = np.zeros((LB, 128, 256), np.float32)
    am_rel = np.zeros((LB, 128, NREL), np.float32)
    vm_rel = np.zeros((LB, 128, NREL), np.float32)
    ovrel = np.zeros((LB, 128, 3, NREL), np.float32)
    cmpm = np.zeros((LB, 128, 2, 128), np.float32)
    qq = np.arange(128)
    for j in range(LB):
        n = NCORES * j + c
        qpos = n * 128 + qq
        cur = qpos // 64
        jb = np.arange(256)[None, :]
        valid = (jb * 64) <= qpos[:, None]
        am = np.where(valid, 0.0, NEGV)
        vm = valid.astype(np.float32)
        for val, fb in ((FORCE_V[0], np.zeros(128, np.int64)), (FORCE_V[1], cur), (FORCE_V[2], cur - 1)):
            ok = fb >= 0
            am[qq[ok], fb[ok]] = val
            vm[qq[ok], fb[ok]] = 0.0
        am_abs[j], vm_abs[j] = am, vm
        for p in range(NREL):
            b = 2 * n - 38 + p
            r = (39 - p) // 2
            if b < 0 or r > 12 + c:
                am_rel[j, :, p] = NEGV
                vm_rel[j, :, p] = 0.0
            else:
                am_rel[j, :, p] = am[:, b]
                vm_rel[j, :, p] = vm[:, b]
        mlast = j // 2
        for mm in range(3):
            m = mlast - 2 + mm
            if m < 0:
                continue
            for p in range(NREL):
                b = 2 * n - 38 + p
                if 0 <= b < 256:
                    ovrel[j, :, mm, p] = ov[m * 128:(m + 1) * 128, b]
        for mm in range(2):
            m = mlast - 1 + mm
            if m < 0:
                continue
            i = m * 128 + np.arange(128)[:, None]
            cmpm[j, :, mm, :] = (16 * i + 31) <= qpos[None, :]
    return am_abs, vm_abs, am_rel, vm_rel, ovrel, cmpm


def bc_mid(ap, n):
    a = [list(x) for x in ap.ap]
    return bass.AP(tensor=ap.tensor, offset=ap.offset, ap=[a[0], [0, n]] + a[1:])


def bc_last(ap, n):
    a = [list(x) for x in ap.ap]
    return bass.AP(tensor=ap.tensor, offset=ap.offset, ap=a + [[0, n]])


def _win_heads(r):
    if r <= 1:
        return 0, 10
    if r <= 4:
        return 2, 8
    if r <= 16:
        return 4, 6
    return 8, 2


WIN_OFF = []
_o = 0
for _r in range(NDYN):
    WIN_OFF.append(_o)
    _o += _win_heads(_r)[1]
WIN_TOT = _o


def win_slot(r, head):
    h0, nh = _win_heads(r)
    assert h0 <= head < h0 + nh
    return WIN_OFF[r] + head - h0


def emit_tables(K, rel_bias_d, bm_d, cnt_d, tscr_d, EA, ECW, ES, r_tab):
    nc, S = K.nc, K.S
    from contextlib import ExitStack
    with ExitStack() as st2:
        A = lambda n, s, d: st2.enter_context(nc.sbuf_tensor(n, s, d))
        rb = A("tb_rb", [32, 12], F32)
        bm = A("tb_bm", [32, DTOT], F32)
        cn = A("tb_cn", [12, DTOT], F32)
        T = A("tb_T", [12, DTOT], F32)
        X = A("tb_X", [128, 17, 128], F32)
        Xb = A("tb_Xb", [128, 17, 128], BF16)
        Jf = A("tb_Jf", [128, 128], F32)
        Jb = A("tb_Jb", [128, 128], BF16)
        ps = st2.enter_context(nc.psum_tensor("tb_ps", [128, 512], F32))
        ps2 = st2.enter_context(nc.psum_tensor("tb_ps2", [128, 512], F32))
        r_in, r_T, r_ps, r_X, r_J, r_scr = Res("tb_in"), Res("tb_T"), Res("tb_ps"), Res("tb_X"), Res("tb_J"), Res("tb_scr")
        S.dma("sp", rb[:], rel_bias_d[:, :], writes=[r_in])
        S.dma("sp", bm[:], bm_d[:, :], writes=[r_in])
        S.dma("sp", cn[:], cnt_d[:, :], writes=[r_in])
        S.op("pool", lambda e: e.memset(Jf[:], 0.0), writes=[r_J])
        S.op("pool", lambda e: e.affine_select(out=Jf[:], in_=K.ones_f[:], pattern=[[1, 128]], compare_op=ALU.is_equal,
                                               fill=0.0, base=-127, channel_multiplier=1), reads=[K.r_const, r_J], writes=[r_J])
        S.op("dve", lambda e: e.tensor_copy(out=Jb[:], in_=Jf[:]), reads=[r_J], writes=[r_J])
        for c0 in range(0, DTOT, 512):
            c1 = min(DTOT, c0 + 512)
            S.op("pe", lambda e, c0=c0, c1=c1: e.matmul(ps[0:12, 0:c1 - c0], lhsT=rb[:], rhs=bm[:, c0:c1], start=True, stop=True),
                 reads=[r_in], writes=[r_ps])
            S.op("act", lambda e, c0=c0, c1=c1: e.activation(out=T[:, c0:c1], in_=ps[0:12, 0:c1 - c0], func=AF.Exp),
                 reads=[r_ps], writes=[r_T])
        S.op("dve", lambda e: e.tensor_tensor(out=T[:], in0=T[:], in1=cn[:], op=ALU.mult), reads=[r_T, r_in], writes=[r_T])
        S.dma("sp", tscr_d[:, :], T[:], reads=[r_T], writes=[r_scr])
        psp = Pool([ps, ps2], "tb_psp")
        for (tab, off, R, h0, nh) in ((EA, 0, 17, 0, 4), (ECW, D_A, 5, 4, 8), (ES, D_A + D_W, 13, 4, 8)):
            for h in range(nh):
                src = bass.AP(tensor=tscr_d.tensor, offset=tscr_d.offset + (h0 + h) * DTOT + off, ap=[[1, 128], [128, R], [1, 128]])
                S.dma("sp", X[:, 0:R, :], src, reads=[r_scr], writes=[r_X])
                S.op("dve", lambda e, R=R: e.tensor_copy(out=Xb[:, 0:R, :], in_=X[:, 0:R, :]), reads=[r_X], writes=[r_X])
                for r0 in range(0, R, 4):
                    r1 = min(R, r0 + 4)
                    pp, rp = psp.get()
                    S.op("pe", lambda e, pp=pp, r0=r0, r1=r1: e.matmul(pp[:, 0:(r1 - r0) * 128], lhsT=Jb[:],
                                                                      rhs=Xb[:, r0:r1, :].rearrange("p r q -> p (r q)"),
                                                                      start=True, stop=True), reads=[r_X, r_J], writes=[rp])
                    S.op("act", lambda e, pp=pp, r0=r0, r1=r1, tab=tab, h=h: e.copy(
                        out=tab[:, r0:r1, h, :], in_=pp[:, 0:(r1 - r0) * 128].rearrange("p (r q) -> p r q", q=128)),
                        reads=[rp], writes=[r_tab])
        S.barrier()


def emit_compress(K, Gkt_d, cmp_pos_d, cmp_w1_d, cmp_w2_d, KC, CV, r_cmp):
    nc, S = K.nc, K.S
    from contextlib import ExitStack
    with ExitStack() as st2:
        A = lambda n, s, d: st2.enter_context(nc.sbuf_tensor(n, s, d))
        P = lambda n, s, d: st2.enter_context(nc.psum_tensor(n, s, d))
        W1 = A("cp_w1", [64, 32, 2, 256], BF16)
        w1s0 = A("cp_w1s0", [64, 8, 256], F32)
        w1s1 = A("cp_w1s1", [64, 8, 256], F32)
        W2 = A("cp_w2", [128, 2, 2, 64], BF16)
        w2s = A("cp_w2s", [128, 2, 2, 64], F32)
        pos64 = A("cp_pos", [64, 64], F32)
        posT = A("cp_posT", [64, 64], BF16)
        c1 = A("cp_c1", [128, 4], F32)
        X0 = A("cp_x0", [64, 4, 2176], BF16)
        X1 = A("cp_x1", [64, 4, 2176], BF16)
        z = A("cp_z", [128, 256], F32)
        z2 = A("cp_z2", [128, 256], F32)
        hid = A("cp_hid", [128, 2, 2, 256], BF16)
        ph0 = P("cp_ph0", [128, 256], F32)
        ph1 = P("cp_ph1", [128, 256], F32)
        po = P("cp_po", [128, 512], F32)
        pc = P("cp_pc", [128, 64], F32)
        r_w1, r_w2, r_pos, r_c1, r_z, r_hid, r_po, r_pc = (Res("cp_w1"), Res("cp_w2"), Res("cp_pos"), Res("cp_c1"),
                                                        Res("cp_z"), Res("cp_hid"), Res("cp_po"), Res("cp_pc"))
        wsp = Pool([w1s0, w1s1], "cp_w1s")
        for kind in range(2):
            for t0 in range(0, 32, 8):
                stg, rs = wsp.get()
                S.dma("sp", stg[:], cmp_w1_d[kind, t0 * 64:(t0 + 8) * 64, :].rearrange("(t d) n -> d t n", d=64), writes=[rs])
                S.op("pool", lambda e, stg=stg, kind=kind, t0=t0: e.tensor_copy(out=W1[:, t0:t0 + 8, kind, :], in_=stg[:]),
                     reads=[rs], writes=[r_w1])
        S.dma("sp", w2s[:], cmp_w2_d.rearrange("k (c p) d -> p k c d", p=128), writes=[r_w2])
        S.op("pool", lambda e: e.tensor_copy(out=W2[:], in_=w2s[:]), reads=[r_w2], writes=[r_w2])
        S.dma("sp", pos64[:], cmp_pos_d.rearrange("k t d -> (k t) d"), writes=[r_pos])
        S.op("pe", lambda e: e.matmul(pc[0:64, 0:64], lhsT=pos64[:], rhs=K.ident_f[0:64, 0:64], start=True, stop=True),
             reads=[r_pos, K.r_const], writes=[r_pc])
        S.op("dve", lambda e: e.tensor_copy(out=posT[:], in_=pc[0:64, 0:64]), reads=[r_pc], writes=[r_pos])
        for kind in range(2):
            for hc in range(2):
                col = kind * 2 + hc
                for t in range(32):
                    S.op("pe", lambda e, kind=kind, hc=hc, t=t, col=col: e.matmul(
                        ph0[:, col:col + 1], lhsT=W1[:, t, kind, hc * 128:(hc + 1) * 128],
                        rhs=posT[:, kind * 32 + t:kind * 32 + t + 1], start=(t == 0), stop=(t == 31)),
                        reads=[r_w1, r_pos], writes=[r_pc])
        S.op("dve", lambda e: e.tensor_copy(out=c1[:], in_=ph0[:, 0:4]), reads=[r_pc], writes=[r_c1])
        for Xt in (X0, X1):
            S.op("pool", lambda e, Xt=Xt: e.memset(Xt[:], 0.0), writes=[r_z])
        S.op("pool", lambda e: e.memset(CV[:], 1.0), writes=[r_cmp])
        xp = Pool([X0, X1], "cp_x")
        xp.res[0].lw = r_z.lw
        xp.res[1].lw = r_z.lw
        php = Pool([ph0, ph1], "cp_ph")
        for m in range(8):
            Xt, rx = xp.get()
            nb = 17 if m < 7 else 16
            for bi in range(nb):
                n = 16 * m + bi
                row = (n + NPADL) * (NKH * 64) + KH_CC * 64
                ntok = 128 if bi < 16 else 16
                src = bass.AP(tensor=Gkt_d.tensor, offset=Gkt_d.offset + row * 128, ap=[[128, 64], [64 * 128, 4], [1, ntok]])
                S.dma("sp", Xt[:, :, bi * 128:bi * 128 + ntok], src, writes=[rx])
            for kind in range(2):
                for hc in range(2):
                    ph, rph = php.get()
                    for t in range(32):
                        rhs = bass.AP(tensor=Xt[:].tensor, offset=Xt[:, kind * 2, t:t + 1].offset,
                                      ap=[list(Xt[:, 0, 0:1].ap[0]), [2176, 2], [16, 128]])
                        S.op("pe", lambda e, ph=ph, t=t, kind=kind, hc=hc, rhs=rhs: e.matmul(
                            ph[:], lhsT=W1[:, t, kind, hc * 128:(hc + 1) * 128], rhs=rhs, start=(t == 0), stop=(t == 31)),
                            reads=[r_w1, rx], writes=[rph])
                    col = kind * 2 + hc
                    S.op("act", lambda e, ph=ph, col=col: e.activation(out=z[:], in_=ph[:], func=AF.Identity,
                                                                         bias=c1[:, col:col + 1], scale=1.0),
                         reads=[rph, r_c1], writes=[r_z])
                    S.op("dve", lambda e: e.tensor_tensor(out=z2[:], in0=z[:], in1=z[:], op=ALU.mult), reads=[r_z], writes=[r_z])
                    S.op("dve", lambda e: e.tensor_scalar(out=z2[:], in0=z2[:], scalar1=0.044715, scalar2=1.0, op0=ALU.mult,
                                                          op1=ALU.add), reads=[r_z], writes=[r_z])
                    S.op("dve", lambda e: e.tensor_tensor(out=z2[:], in0=z2[:], in1=z[:], op=ALU.mult), reads=[r_z], writes=[r_z])
                    S.op("act", lambda e: e.activation(out=z2[:], in_=z2[:], func=AF.Tanh, scale=0.7978845608028654),
                         reads=[r_z], writes=[r_z])
                    S.op("dve", lambda e: e.tensor_scalar(out=z2[:], in0=z2[:], scalar1=0.5, scalar2=0.5, op0=ALU.mult,
                                                          op1=ALU.add), reads=[r_z], writes=[r_z])
                    S.op("dve", lambda e, kind=kind, hc=hc: e.tensor_tensor(out=hid[:, kind, hc, :], in0=z2[:], in1=z[:],
                                                                            op=ALU.mult), reads=[r_z], writes=[r_hid])
            for hc in range(2):
                S.op("pe", lambda e, hc=hc: e.matmul(po[0:64, 0:256], lhsT=W2[:, 0, hc, :], rhs=hid[:, 0, hc, :],
                                                     start=(hc == 0), stop=(hc == 1)), reads=[r_w2, r_hid], writes=[r_po])
            S.op("act", lambda e, m=m: e.copy(out=KC[:, :, m * 128:(m + 1) * 128],
                                              in_=po[0:64, 0:256].rearrange("p (g i) -> p g i", g=2)), reads=[r_po], writes=[r_cmp])
            for g in range(2):
                for hc in range(2):
                    S.op("pe", lambda e, hc=hc, g=g: e.matmul(po[:, 256 + g * 64:256 + (g + 1) * 64],
                                                              lhsT=hid[:, 1, hc, g * 128:(g + 1) * 128], rhs=W2[:, 1, hc, :],
                                                              start=(hc == 0), stop=(hc == 1)), reads=[r_w2, r_hid], writes=[r_po])
            S.op("dve", lambda e, m=m: e.tensor_copy(out=CV[:, m, :, 0:64],
                                                     in_=po[:, 256:384].rearrange("p (g d) -> p g d", g=2)), reads=[r_po], writes=[r_cmp])
        S.barrier()


def alloc_attn(K, st2):
    nc = K.nc
    K.uid = getattr(K, "uid", 0) + 1
    sfx = "_u%d" % K.uid
    A = lambda n, s, d: st2.enter_context(nc.sbuf_tensor(n + sfx, s, d))
    P = lambda n, s, d: st2.enter_context(nc.psum_tensor(n + sfx, s, d))
    W = Ctx()
    W.KWIN = A("at_kwin", [64, WIN_TOT, 128], BF16)
    W.VWIN = A("at_vwin", [128, WIN_TOT, 65], BF16)
    W.r_kwin = [Res("at_kwin%d" % r) for r in range(NDYN)]
    W.r_vwin = [Res("at_vwin%d" % r) for r in range(NDYN)]
    W.QB = Pool([A("at_qb0", [64, NQH, 128], BF16), A("at_qb1", [64, NQH, 128], BF16)], "at_qb")
    W.KSF = Pool([A("at_ksf0", [64, 8, 2, 128], BF16), A("at_ksf1", [64, 8, 2, 128], BF16)], "at_ksf")
    W.VSF = Pool([A("at_vsf0", [128, 8, 2, 65], BF16), A("at_vsf1", [128, 8, 2, 65], BF16)], "at_vsf")
    W.EX = Pool([A("at_ex%d" % i, [128, 512], BF16) for i in range(3)], "at_ex")
    W.PP = Pool([A("at_pp%d" % i, [128, 512], BF16) for i in range(3)], "at_pp")
    W.TM = Pool([A("at_tm%d" % i, [128, 512], BF16) for i in range(2)], "at_tm")
    W.PS = Pool([P("at_ps0", [128, 512], F32), P("at_ps1", [128, 512], F32)], "at_ps")
    W.SX = Pool([A("at_sx%d" % i, [128, 2, 64], BF16) for i in range(4)], "at_sx")
    W.PM = P("at_pm", [128, 512], F32)
    W.r_pm = [Res("at_pm%d" % i) for i in range(4)]
    W.pm_i = 0
    W.PO = [P("at_po%d" % i, [128, 512], F32) for i in range(4)]
    W.r_po = [Res("at_po%d" % i) for i in range(4)]
    W.PX = P("at_px", [128, 1024], BF16)
    W.r_px = Res("at_px")
    W.gat = A("at_gat", [128, 24], F32); W.r_gat = Res("at_gat")
    W.amA = A("at_amA", [128, 256], F32); W.vmA = A("at_vmA", [128, 256], F32)
    W.amR = A("at_amR", [128, NREL], F32); W.vmR = A("at_vmR", [128, NREL], F32)
    W.ovr = A("at_ovr", [128, 3, NREL], BF16); W.ovr_s = A("at_ovrs", [128, 3, NREL], F32)
    W.cmk = A("at_cmk", [128, 2, 128], BF16); W.cmk_s = A("at_cmks", [128, 2, 128], F32)
    W.r_pc = Res("at_pc")
    W.OA = A("at_oa", [128, 4, 65], F32); W.OB = A("at_ob", [128, 4, 65], F32)
    W.OW = A("at_ow", [128, 8, 65], F32); W.OSL = A("at_osl", [128, 8, 65], F32)
    W.OCM = A("at_ocm", [128, 8, 64], F32)
    W.r_o = Res("at_o")
    W.den = A("at_den", [128, 8], F32); W.rden = A("at_rden", [128, 8], F32)
    W.scA = A("at_scA", [128, 256], F32); W.scR = A("at_scR", [128, NREL], F32)
    W.wk = A("at_wk", [128, 256], F32)
    W.m8a = A("at_m8a", [128, 8], F32); W.m8b = A("at_m8b", [128, 8], F32); W.thr = A("at_thr", [128, 1], F32)
    W.selA = A("at_selA", [128, 2, 256], BF16); W.selR = A("at_selR", [128, 2, NREL], BF16)
    W.r_sc = Res("at_sc"); W.r_sel = Res("at_sel")
    W.mix = A("at_mix", [128, 16, 64], F32); W.mixb = A("at_mixb", [128, 1024], BF16)
    W.r_mix = Res("at_mix")
    W.tmp8 = A("at_tmp8", [128, 8, 64], F32)
    W.sg = A("at_sg", [128, 24], F32)
    return W


def emit_attn_block(K, W, j, T, qT_d, Gkt_d, Gv_d, Gktw_d, Gvw_d, gates_d, pc_d, mixT, r_mixT, tcol):
    nc, S = K.nc, K.S
    QB, r_qb = W.QB.get()
    S.dma("sp", QB[:], qT_d[j, :, :].rearrange("(h d) t -> d h t", d=64), writes=[r_qb])
    for r in range(NDYN):
        h0, nh = _win_heads(r)
        row = NCORES * j - r + NPADL
        so = WIN_OFF[r]
        ksrc = Gktw_d[row * (NKH * 64) + h0 * 64:row * (NKH * 64) + (h0 + nh) * 64, :].rearrange("(h d) t -> d h t", d=64)
        S.dma("pool", W.KWIN[:, so:so + nh, :], ksrc, writes=[W.r_kwin[r]])
        vsrc = Gvw_d[row * 128:(row + 1) * 128, h0 * 65:(h0 + nh) * 65].rearrange("p (h d) -> p h d", d=65)
        S.dma("pool", W.VWIN[:, so:so + nh, :], vsrc, writes=[W.r_vwin[r]])
    S.dma("sp", W.gat[:], gates_d[j * 128:(j + 1) * 128, :], writes=[W.r_gat])
    S.dma("sp", W.amA[:], pc_d["am_abs"][j, :, :], writes=[W.r_pc])
    S.dma("sp", W.vmA[:], pc_d["vm_abs"][j, :, :], writes=[W.r_pc])
    S.dma("sp", W.amR[:], pc_d["am_rel"][j, :, :], writes=[W.r_pc])
    S.dma("sp", W.vmR[:], pc_d["vm_rel"][j, :, :], writes=[W.r_pc])
    S.dma("sp", W.ovr_s[:], pc_d["ovrel"][j, :, :, :], writes=[W.r_pc])
    S.dma("sp", W.cmk_s[:], pc_d["cmpm"][j, :, :, :], writes=[W.r_pc])
    S.op("pool", lambda e: e.tensor_copy(out=W.ovr[:], in_=W.ovr_s[:]), reads=[W.r_pc], writes=[W.r_pc])
    S.op("pool", lambda e: e.tensor_copy(out=W.cmk[:], in_=W.cmk_s[:]), reads=[W.r_pc], writes=[W.r_pc])

    def unit(qk_list, mask_fn, pv_list, extra_reads=()):
        ps, rps = W.PS.get()
        for (c0, ncol, lh, rh, rd) in qk_list:
            S.op("pe", lambda e, ps=ps, c0=c0, ncol=ncol, lh=lh, rh=rh: e.matmul(ps[:, c0:c0 + ncol], lhsT=lh, rhs=rh,
                                                                                   start=True, stop=True),
                 reads=list(rd), writes=[rps])
        ex, rex = W.EX.get()
        S.op("act", lambda e, ex=ex, ps=ps: e.activation(out=ex[:], in_=ps[:], func=AF.Exp, scale=0.125),
             reads=[rps], writes=[rex])
        cur, rcur = ex, rex
        if mask_fn is not None:
            for (in1, rd) in mask_fn():
                pp, rpp = W.PP.get()
                S.op("dve", lambda e, pp=pp, cur=cur, in1=in1: e.tensor_tensor(
                    out=pp[:].rearrange("p (h q) -> p h q", q=128), in0=cur[:].rearrange("p (h q) -> p h q", q=128),
                    in1=in1, op=ALU.mult), reads=[rcur] + list(rd), writes=[rpp])
                cur, rcur = pp, rpp
        for (out_ap, c0, c1, rh, st_, sp_, r_out, rd) in pv_list:
            S.op("pe", lambda e, out_ap=out_ap, cur=cur, c0=c0, c1=c1, rh=rh, st_=st_, sp_=sp_: e.matmul(
                out_ap, lhsT=cur[:, c0:c1], rhs=rh, start=st_, stop=sp_), reads=[rcur] + list(rd), writes=[r_out])

    for r in range(17):
        qk = [(h * 128, 128, W.KWIN[:, win_slot(r, KH_A + h), :], QB[:, h, :], (W.r_kwin[r], r_qb)) for h in range(4)]
        pv = [(W.PO[0][:, h * 65:(h + 1) * 65], h * 128, (h + 1) * 128, W.VWIN[:, win_slot(r, VH_A + h), :],
               r == 0 and h == 0, r == 16, W.r_po[0], (W.r_vwin[r],)) for h in range(4)]
        unit(qk, lambda r=r: [(T.EA[:, r, :, :], (T.r_tab,))], pv)
    for r in range(2):
        qk = [(g * 256, 256, W.KWIN[:, win_slot(r, KH_B + g), :], QB[:, 4 + 2 * g:6 + 2 * g, :].rearrange("p h q -> p (h q)"),
               (W.r_kwin[r], r_qb)) for g in range(2)]
        pv = [(W.PO[1][:, h * 65:(h + 1) * 65], h * 128, (h + 1) * 128, W.VWIN[:, win_slot(r, VH_B + h // 2), :],
               r == 0 and h == 0, r == 1, W.r_po[1], (W.r_vwin[r],)) for h in range(4)]
        unit(qk, lambda r=r: [(bc_mid(T.MB[:, r, :], 4), (T.r_tab,))], pv)
    for g in range(2):
        for r in range(5):
            qk = [(0, 512, W.KWIN[:, win_slot(r, KH_CW + g), :], QB[:, 8 + 4 * g:12 + 4 * g, :].rearrange("p h q -> p (h q)"),
                   (W.r_kwin[r], r_qb))]
            pv = [(W.PO[2 + g][:, h * 65:(h + 1) * 65], h * 128, (h + 1) * 128, W.VWIN[:, win_slot(r, VH_CW + g), :],
                   r == 0 and h == 0, r == 4, W.r_po[2 + g], (W.r_vwin[r],)) for h in range(4)]
            unit(qk, lambda r=r, g=g: [(T.ECW[:, r, 4 * g:4 * g + 4, :], (T.r_tab,))], pv)
    S.op("dve", lambda e: e.tensor_copy(out=W.OA[:].rearrange("p h d -> p (h d)"), in_=W.PO[0][:, 0:260]), reads=[W.r_po[0]], writes=[W.r_o])
    S.op("dve", lambda e: e.tensor_copy(out=W.OB[:].rearrange("p h d -> p (h d)"), in_=W.PO[1][:, 0:260]), reads=[W.r_po[1]], writes=[W.r_o])
    for g in range(2):
        S.op("dve", lambda e, g=g: e.tensor_copy(out=W.OW[:, 4 * g:4 * g + 4, :].rearrange("p h d -> p (h d)"),
                                                 in_=W.PO[2 + g][:, 0:260]), reads=[W.r_po[2 + g]], writes=[W.r_o])

    mlast = j // 2
    for g in range(2):
        for m in range(mlast + 1):
            mm2 = m - (mlast - 1)
            mm3 = m - (mlast - 2)
            qk = [(0, 512, T.KC[:, g, m * 128:(m + 1) * 128], QB[:, 8 + 4 * g:12 + 4 * g, :].rearrange("p h q -> p (h q)"),
                   (T.r_cmp, r_qb))]
            pv = []
            for h in range(4):
                pv.append((W.PO[h][:, 0:65], h * 128, (h + 1) * 128, T.CV[:, m, g, :], m == 0, m == mlast, W.r_po[h], (T.r_cmp,)))
                pv.append((W.PO[h][:, 65:321], h * 128, (h + 1) * 128, T.OV[:, m, :], False, m == mlast, W.r_po[h], (T.r_tab,)))
                if mm3 >= 0:
                    pv.append((W.PO[h][:, 321:321 + NREL], h * 128, (h + 1) * 128, W.ovr[:, mm3, :], False,
                               m == mlast, W.r_po[h], (W.r_pc,)))
            mf = None
            if mm2 >= 0:
                mf = (lambda mm2=mm2: [(bc_mid(W.cmk[:, mm2, :], 4), (W.r_pc,))])
            unit(qk, mf, pv)
        for h in range(4):
            S.op("dve", lambda e, h=h: e.tensor_scalar(out=W.den[:, h:h + 1], in0=W.PO[h][:, 64:65], scalar1=1e-30, scalar2=None,
                                                       op0=ALU.max), reads=[W.r_po[h]], writes=[W.r_sc])
        S.op("dve", lambda e: e.reciprocal(out=W.rden[:, 0:4], in_=W.den[:, 0:4]), reads=[W.r_sc], writes=[W.r_sc])
        for h in range(4):
            S.op("act", lambda e, h=h, g=g: e.activation(out=W.OCM[:, 4 * g + h, :], in_=W.PO[h][:, 0:64], func=AF.Copy,
                                                         scale=W.rden[:, h:h + 1]), reads=[W.r_po[h], W.r_sc], writes=[W.r_o])
            if h == 0:
                S.op("dve", lambda e: e.tensor_scalar(out=W.scA[:], in0=W.PO[0][:, 65:321], scalar1=W.rden[:, 0:1], scalar2=None,
                                                      op0=ALU.mult), reads=[W.r_po[0], W.r_sc], writes=[W.r_sc])
                S.op("dve", lambda e: e.tensor_scalar(out=W.scR[:], in0=W.PO[0][:, 321:321 + NREL], scalar1=W.rden[:, 0:1],
                                                      scalar2=None, op0=ALU.mult), reads=[W.r_po[0], W.r_sc], writes=[W.r_sc])
            else:
                S.op("dve", lambda e, h=h: e.scalar_tensor_tensor(out=W.scA[:], in0=W.PO[h][:, 65:321], scalar=W.rden[:, h:h + 1],
                                                                  in1=W.scA[:], op0=ALU.mult, op1=ALU.add),
                     reads=[W.r_po[h], W.r_sc], writes=[W.r_sc])
                S.op("dve", lambda e, h=h: e.scalar_tensor_tensor(out=W.scR[:], in0=W.PO[h][:, 321:321 + NREL],
                                                                  scalar=W.rden[:, h:h + 1], in1=W.scR[:], op0=ALU.mult,
                                                                  op1=ALU.add), reads=[W.r_po[h], W.r_sc], writes=[W.r_sc])
        for (sc, vm, am) in ((W.scA, W.vmA, W.amA), (W.scR, W.vmR, W.amR)):
            S.op("dve", lambda e, sc=sc, vm=vm: e.tensor_tensor(out=sc[:], in0=sc[:], in1=vm[:], op=ALU.mult),
                 reads=[W.r_sc, W.r_pc], writes=[W.r_sc])
            S.op("dve", lambda e, sc=sc, am=am: e.tensor_tensor(out=sc[:], in0=sc[:], in1=am[:], op=ALU.add),
                 reads=[W.r_sc, W.r_pc], writes=[W.r_sc])
        S.op("dve", lambda e: e.max(out=W.m8a[:], in_=W.scA[:]), reads=[W.r_sc], writes=[W.r_sc])
        S.op("dve", lambda e: e.match_replace(out=W.wk[:], in_to_replace=W.m8a[:], in_values=W.scA[:], imm_value=-3.0e38),
             reads=[W.r_sc], writes=[W.r_sc])
        S.op("dve", lambda e: e.max(out=W.m8b[:], in_=W.wk[:]), reads=[W.r_sc], writes=[W.r_sc])
        S.op("dve", lambda e: e.tensor_scalar(out=W.thr[:], in0=W.m8b[:, 7:8], scalar1=-1.0e29, scalar2=None, op0=ALU.max),
             reads=[W.r_sc], writes=[W.r_sc])
        S.op("dve", lambda e, g=g: e.tensor_scalar(out=W.selA[:, g, :], in0=W.scA[:], scalar1=W.thr[:, 0:1], scalar2=None,
                                                   op0=ALU.is_ge), reads=[W.r_sc], writes=[W.r_sel])
        S.op("dve", lambda e, g=g: e.tensor_scalar(out=W.selR[:, g, :], in0=W.scR[:], scalar1=W.thr[:, 0:1], scalar2=None,
                                                   op0=ALU.is_ge), reads=[W.r_sc], writes=[W.r_sel])

    nfar = max(0, NCORES * j - 12)
    first = {0: True, 1: True}

    def maskgen(sel_ap):
        slot = W.pm_i % 4
        W.pm_i += 1
        rpm = W.r_pm[slot]
        pm = W.PM[:, slot * 128:(slot + 1) * 128]
        sx, rsx = W.SX.get()
        S.op("pool", lambda e, sx=sx, sel_ap=sel_ap: e.tensor_copy(out=sx[:], in_=sel_ap), reads=[W.r_sel], writes=[rsx])
        S.op("pe", lambda e, pm=pm, sx=sx: e.matmul(pm, lhsT=sx[:].rearrange("p b k -> p (b k)"), rhs=K.ident_b[:], start=True, stop=True),
             reads=[rsx, K.r_const], writes=[rpm])
        return pm, rpm

    for c0 in range(0, nfar, 8):
        c1 = min(nfar, c0 + 8)
        KS, rks = W.KSF.get()
        VS, rvs = W.VSF.get()
        for t in range(c0, c1):
            row = t + NPADL
            S.dma("sp", KS[:, t - c0, :, :], Gkt_d[row * (NKH * 64) + KH_CS * 64:row * (NKH * 64) + (KH_CS + 2) * 64, :].rearrange("(h d) t -> d h t", d=64), writes=[rks])
            S.dma("sp", VS[:, t - c0, :, :], Gv_d[row * 128:(row + 1) * 128, VH_CS * 65:(VH_CS + 2) * 65].rearrange("p (h d) -> p h d", d=65), writes=[rvs])
        for t in range(c0, c1):
            for g in range(2):
                pm, rpm = maskgen(bc_last(W.selA[:, g, 2 * t:2 * t + 2], 64))
                qk = [(0, 512, KS[:, t - c0, g, :], QB[:, 8 + 4 * g:12 + 4 * g, :].rearrange("p h q -> p (h q)"), (rks, r_qb))]
                pv = [(W.PO[g][:, h * 65:(h + 1) * 65], h * 128, (h + 1) * 128, VS[:, t - c0, g, :], first[g] and h == 0, False, W.r_po[g], (rvs,))
                      for h in range(4)]
                first[g] = False
                unit(qk, lambda pm=pm, rpm=rpm: [(bc_mid(pm, 4), (rpm,))], pv)
    for r in range(NDYN - 1, -1, -1):
        for g in range(2):
            pm, rpm = maskgen(bc_last(W.selR[:, g, 38 - 2 * r:40 - 2 * r], 64))
            qk = [(0, 512, W.KWIN[:, win_slot(r, KH_CS + g), :], QB[:, 8 + 4 * g:12 + 4 * g, :].rearrange("p h q -> p (h q)"),
                   (W.r_kwin[r], r_qb))]
            pv = [(W.PO[g][:, h * 65:(h + 1) * 65], h * 128, (h + 1) * 128, W.VWIN[:, win_slot(r, VH_CS + g), :], first[g] and h == 0, r == 0,
                   W.r_po[g], (W.r_vwin[r],)) for h in range(4)]
            first[g] = False
            if r <= 12:
                def mf(pm=pm, rpm=rpm, r=r, g=g):
                    tm, rtm = W.TM.get()
                    S.op("dve", lambda e: e.tensor_tensor(out=tm[:].rearrange("p (h q) -> p h q", q=128),
                                                          in0=T.ES[:, r, 4 * g:4 * g + 4, :], in1=bc_mid(pm, 4), op=ALU.mult),
                         reads=[T.r_tab, rpm], writes=[rtm])
                    return [(tm[:].rearrange("p (h q) -> p h q", q=128), (rtm,))]
            else:
                def mf(pm=pm, rpm=rpm):
                    return [(bc_mid(pm, 4), (rpm,))]
            unit(qk, mf, pv)
    for g in range(2):
        S.op("dve", lambda e, g=g: e.tensor_copy(out=W.OSL[:, 4 * g:4 * g + 4, :].rearrange("p h d -> p (h d)"),
                                                 in_=W.PO[g][:, 0:260]), reads=[W.r_po[g]], writes=[W.r_o])

    S.op("dve", lambda e: e.reciprocal(out=W.rden[:, 0:4], in_=W.OA[:, :, 64]), reads=[W.r_o], writes=[W.r_sc])
    S.op("dve", lambda e: e.tensor_tensor(out=W.mix[:, 0:4, :], in0=W.OA[:, :, 0:64], in1=bc_last(W.rden[:, 0:4], 64), op=ALU.mult),
         reads=[W.r_o, W.r_sc], writes=[W.r_mix])
    S.op("dve", lambda e: e.tensor_tensor(out=W.den[:, 0:4], in0=W.OB[:, :, 64], in1=T.esink[:], op=ALU.add),
         reads=[W.r_o, T.r_tab], writes=[W.r_sc])
    S.op("dve", lambda e: e.reciprocal(out=W.rden[:, 0:4], in_=W.den[:, 0:4]), reads=[W.r_sc], writes=[W.r_sc])
    S.op("dve", lambda e: e.tensor_tensor(out=W.mix[:, 4:8, :], in0=W.OB[:, :, 0:64], in1=bc_last(W.rden[:, 0:4], 64), op=ALU.mult),
         reads=[W.r_o, W.r_sc], writes=[W.r_mix])
    S.op("act", lambda e: e.activation(out=W.sg[:], in_=W.gat[:], func=AF.Exp, scale=-1.0), reads=[W.r_gat], writes=[W.r_sc])
    S.op("dve", lambda e: e.tensor_scalar(out=W.sg[:], in0=W.sg[:], scalar1=1.0, scalar2=None, op0=ALU.add), reads=[W.r_sc], writes=[W.r_sc])
    S.op("dve", lambda e: e.reciprocal(out=W.sg[:], in_=W.sg[:]), reads=[W.r_sc], writes=[W.r_sc])
    sg3 = W.sg[:].rearrange("p (h c) -> p h c", c=3)
    S.op("dve", lambda e: e.tensor_tensor(out=W.mix[:, 8:16, :], in0=W.OCM[:], in1=bc_last(sg3[:, :, 0], 64), op=ALU.mult),
         reads=[W.r_o, W.r_sc], writes=[W.r_mix])
    for (O, ci) in ((W.OSL, 1), (W.OW, 2)):
        S.op("dve", lambda e, O=O: e.reciprocal(out=W.rden[:], in_=O[:, :, 64]), reads=[W.r_o], writes=[W.r_sc])
        S.op("dve", lambda e, ci=ci: e.tensor_tensor(out=W.rden[:], in0=W.rden[:], in1=sg3[:, :, ci], op=ALU.mult),
             reads=[W.r_sc], writes=[W.r_sc])
        S.op("dve", lambda e, O=O: e.tensor_tensor(out=W.tmp8[:], in0=O[:, :, 0:64], in1=bc_last(W.rden[:], 64), op=ALU.mult),
             reads=[W.r_o, W.r_sc], writes=[W.r_sc])
        S.op("dve", lambda e: e.tensor_tensor(out=W.mix[:, 8:16, :], in0=W.mix[:, 8:16, :], in1=W.tmp8[:], op=ALU.add),
             reads=[W.r_sc, W.r_mix], writes=[W.r_mix])
    S.op("act", lambda e: e.copy(out=W.mixb[:], in_=W.mix[:].rearrange("p h d -> p (h d)")), reads=[W.r_mix], writes=[W.r_mix])
    for kc in range(8):
        S.op("pe", lambda e, kc=kc: e.transpose(out=W.PX[:, kc * 128:(kc + 1) * 128], in_=W.mixb[:, kc * 128:(kc + 1) * 128],
                                                identity=K.ident_b[:]), reads=[W.r_mix, K.r_const], writes=[W.r_px])
    S.op("act", lambda e: e.copy(out=mixT[:, :, tcol * 128:(tcol + 1) * 128], in_=W.PX[:].rearrange("p (k q) -> p k q", q=128)),
         reads=[W.r_px], writes=[r_mixT])


def emit_dense_group(K, T, grp, mixT, r_mixT, x_d, xout_d, w_out_d, w1_d, w2_d, lng_d, lnb_d):
    nc, S = K.nc, K.S
    from contextlib import ExitStack
    with ExitStack() as st2:
        K.uid = getattr(K, "uid", 0) + 1
        sfx = "_u%d" % K.uid
        A = lambda n, s, d: st2.enter_context(nc.sbuf_tensor(n + sfx, s, d))
        P = lambda n, s, d: st2.enter_context(nc.psum_tensor(n + sfx, s, d))
        WO = A("dn_wo", [128, 8, 1024], BF16)
        stg = Pool([A("dn_stg0", [128, 1024], F32)], "dn_stg")
        LG = A("dn_lg", [128, 2, 1024], F32)
        LBt = A("dn_lb", [128, 2, 1024], F32)
        G12 = A("dn_g12", [128, 2, 1024], F32)
        dg = A("dn_dg", [128, 128], F32)
        xt = Pool([A("dn_x0", [128, 1024], F32)], "dn_x")
        y = A("dn_y", [128, 1024], F32)
        junk = A("dn_junk", [128, 1024], BF16)
        x1 = [A("dn_x1_%d" % i, [128, 1024], F32) for i in range(4)]
        r_x1 = [Res("dn_x1_%d" % i) for i in range(4)]
        acc = [A("dn_acc%d" % i, [128, 1024], F32) for i in range(4)]
        r_acc = [Res("dn_acc%d" % i) for i in range(4)]
        xb = A("dn_xb", [128, 1024], BF16)
        h2T = A("dn_h2T", [128, 8, 512], BF16)
        uT = Pool([A("dn_uT0", [128, 4, 128], BF16), A("dn_uT1", [128, 4, 128], BF16)], "dn_uT")
        t1 = A("dn_t1", [128, 512], F32)
        W1s = Pool([A("dn_w1s0", [128, 8, 512], BF16), A("dn_w1s1", [128, 8, 512], BF16)], "dn_w1s")
        W2s = Pool([A("dn_w2s0", [128, 4, 1024], BF16), A("dn_w2s1", [128, 4, 1024], BF16)], "dn_w2s")
        st4 = A("dn_st4", [128, 8], F32)
        pf = [P("dn_pf0", [128, 512], F32), P("dn_pf1", [128, 512], F32)]
        r_pf = Res("dn_pf")
        pt = P("dn_pt", [128, 1024], BF16)
        r_pt = Res("dn_pt")
        pu = Pool([P("dn_pu0", [128, 512], F32), P("dn_pu1", [128, 512], F32)], "dn_pu")
        pg = P("dn_pg", [128, 512], F32)
        r_pg = Res("dn_pg")
        r_wo, r_ln, r_g, r_y, r_xb, r_h2, r_t1, r_st = (Res("dn_wo"), Res("dn_ln"), Res("dn_g"), Res("dn_y"), Res("dn_xb"),
                                                       Res("dn_h2"), Res("dn_t1"), Res("dn_st"))
        load_cast_weight(K, WO, w_out_d, 8, 1024, r_wo, stg)
        for i in range(2):
            S.dma("sp", LG[:, i, :], bass.AP(tensor=lng_d.tensor, offset=lng_d[i:i + 1, :].offset, ap=[[0, 128], [1, 1024]]), writes=[r_ln])
            S.dma("sp", LBt[:, i, :], bass.AP(tensor=lnb_d.tensor, offset=lnb_d[i:i + 1, :].offset, ap=[[0, 128], [1, 1024]]), writes=[r_ln])
        for i, base in ((0, 16), (1, 40)):
            for half in range(2):
                for kk in range(4):
                    kc = half * 4 + kk
                    S.op("dve", lambda e, kc=kc, base=base: e.tensor_scalar(out=dg[:], in0=K.ident_f[:], scalar1=T.mod1[:, base + kc:base + kc + 1],
                                                                            scalar2=None, op0=ALU.mult), reads=[K.r_const, T.r_mod], writes=[r_g])
                    S.op("pe", lambda e, kk=kk: e.matmul(pg[:, kk * 128:(kk + 1) * 128], lhsT=K.ones_f[:], rhs=dg[:], start=True, stop=True),
                         reads=[r_g, K.r_const], writes=[r_pg])
                S.op("act", lambda e, i=i, half=half: e.copy(out=G12[:, i, half * 512:(half + 1) * 512], in_=pg[:]), reads=[r_pg], writes=[r_g])

        def layer_norm_to(dst, r_dst, src, r_src, li):
            S.op("act", lambda e: e.activation(out=junk[:], in_=src[:], func=AF.Copy, accum_out=st4[:, 0:1]), reads=[r_src], writes=[r_st])
            S.op("act", lambda e: e.activation(out=junk[:], in_=src[:], func=AF.Square, accum_out=st4[:, 1:2]), reads=[r_src], writes=[r_st])
            S.op("dve", lambda e: e.tensor_scalar(out=st4[:, 2:3], in0=st4[:, 0:1], scalar1=1.0 / D, scalar2=None, op0=ALU.mult),
                 reads=[r_st], writes=[r_st])
            S.op("dve", lambda e: e.tensor_tensor(out=st4[:, 3:4], in0=st4[:, 2:3], in1=st4[:, 2:3], op=ALU.mult), reads=[r_st], writes=[r_st])
            S.op("dve", lambda e: e.scalar_tensor_tensor(out=st4[:, 4:5], in0=st4[:, 1:2], scalar=1.0 / D, in1=st4[:, 3:4], op0=ALU.mult,
                                                         op1=ALU.subtract), reads=[r_st], writes=[r_st])
            S.op("dve", lambda e: e.tensor_scalar(out=st4[:, 6:7], in0=st4[:, 4:5], scalar1=LN_EPS, scalar2=None, op0=ALU.add),
                 reads=[r_st], writes=[r_st])
            S.op("act", lambda e: e.activation(out=st4[:, 7:8], in_=st4[:, 6:7], func=AF.Sqrt), reads=[r_st], writes=[r_st])
            S.op("dve", lambda e: e.reciprocal(out=st4[:, 5:6], in_=st4[:, 7:8]), reads=[r_st], writes=[r_st])
            S.op("dve", lambda e: e.tensor_scalar(out=dst[:], in0=src[:], scalar1=st4[:, 2:3], scalar2=st4[:, 5:6], op0=ALU.subtract,
                                                  op1=ALU.mult), reads=[r_src, r_st], writes=[r_dst])
            S.op("dve", lambda e: e.tensor_tensor(out=dst[:], in0=dst[:], in1=LG[:, li, :], op=ALU.mult), reads=[r_dst, r_ln], writes=[r_dst])
            S.op("dve", lambda e: e.tensor_tensor(out=dst[:], in0=dst[:], in1=LBt[:, li, :], op=ALU.add), reads=[r_dst, r_ln], writes=[r_dst])

        for tb in range(4):
            b = grp * 4 + tb
            xx, rx = xt.get()
            S.dma("sp", xx[:], x_d[b * 128:(b + 1) * 128, :], writes=[rx])
            for half in range(2):
                for kc in range(8):
                    S.op("pe", lambda e, half=half, kc=kc, tb=tb: e.matmul(pf[half][:], lhsT=mixT[:, kc, tb * 128:(tb + 1) * 128],
                                                                          rhs=WO[:, kc, half * 512:(half + 1) * 512],
                                                                          start=(kc == 0), stop=(kc == 7)), reads=[r_mixT, r_wo], writes=[r_pf])
            for half in range(2):
                hs = slice(half * 512, (half + 1) * 512)
                S.op("dve", lambda e, half=half, hs=hs: e.tensor_tensor(out=y[:, hs], in0=pf[half][:], in1=G12[:, 0, hs], op=ALU.mult),
                     reads=[r_pf, r_g], writes=[r_y])
            S.op("dve", lambda e, xx=xx: e.scalar_tensor_tensor(out=y[:], in0=xx[:], scalar=DN_ALPHA, in1=y[:], op0=ALU.mult, op1=ALU.add),
                 reads=[rx, r_y], writes=[r_y])
            layer_norm_to(x1[tb], r_x1[tb], y, r_y, 0)
            S.op("act", lambda e, tb=tb: e.copy(out=xb[:], in_=x1[tb][:]), reads=[r_x1[tb]], writes=[r_xb])
            for kc in range(8):
                S.op("pe", lambda e, kc=kc: e.transpose(out=pt[:, kc * 128:(kc + 1) * 128], in_=xb[:, kc * 128:(kc + 1) * 128],
                                                        identity=K.ident_b[:]), reads=[r_xb, K.r_const], writes=[r_pt])
            for kc in range(8):
                S.op("act", lambda e, kc=kc, tb=tb: e.activation(out=h2T[:, kc, tb * 128:(tb + 1) * 128], in_=pt[:, kc * 128:(kc + 1) * 128],
                                                                 func=AF.Identity, bias=T.modT[:, 24 + kc:25 + kc],
                                                                 scale=T.mod1[:, 32 + kc:33 + kc]), reads=[r_pt, T.r_mod], writes=[r_h2])
        for hg in range(8):
            w1, rw1 = W1s.get()
            w2, rw2 = W2s.get()
            for kc in range(8):
                sg_, rs_ = stg.get()
                S.dma("sp", sg_[:, 0:512], w1_d[kc * 128:(kc + 1) * 128, hg * 512:(hg + 1) * 512], writes=[rs_])
                S.op("pool", lambda e, sg_=sg_, w1=w1, kc=kc: e.tensor_copy(out=w1[:, kc, :], in_=sg_[:, 0:512]), reads=[rs_], writes=[rw1])
            for hh in range(4):
                sg_, rs_ = stg.get()
                r0 = (hg * 4 + hh) * 128
                S.dma("sp", sg_[:], w2_d[r0:r0 + 128, :], writes=[rs_])
                S.op("pool", lambda e, sg_=sg_, w2=w2, hh=hh: e.tensor_copy(out=w2[:, hh, :], in_=sg_[:]), reads=[rs_], writes=[rw2])
            for tb in range(4):
                pp, rpp = pu.get()
                for hh in range(4):
                    for kc in range(8):
                        S.op("pe", lambda e, pp=pp, hh=hh, kc=kc, tb=tb, w1=w1: e.matmul(
                            pp[:, hh * 128:(hh + 1) * 128], lhsT=w1[:, kc, hh * 128:(hh + 1) * 128], rhs=h2T[:, kc, tb * 128:(tb + 1) * 128],
                            start=(kc == 0), stop=(kc == 7)), reads=[rw1, r_h2], writes=[rpp])
                S.op("act", lambda e, pp=pp: e.activation(out=t1[:], in_=pp[:], func=AF.Relu), reads=[rpp], writes=[r_t1])
                u, ru = uT.get()
                S.op("dve", lambda e, u=u: e.tensor_tensor(out=u[:].rearrange("p h q -> p (h q)"), in0=t1[:], in1=t1[:], op=ALU.mult),
                     reads=[r_t1], writes=[ru])
                for half in range(2):
                    for hh in range(4):
                        S.op("pe", lambda e, half=half, hh=hh, u=u, w2=w2: e.matmul(pf[half][:], lhsT=u[:, hh, :],
                                                                                   rhs=w2[:, hh, half * 512:(half + 1) * 512],
                                                                                   start=(hh == 0), stop=(hh == 3)), reads=[ru, rw2], writes=[r_pf])
                for half in range(2):
                    hs = slice(half * 512, (half + 1) * 512)
                    if hg == 0:
                        S.op("dve", lambda e, half=half, hs=hs, tb=tb: e.tensor_copy(out=acc[tb][:, hs], in_=pf[half][:]),
                             reads=[r_pf], writes=[r_acc[tb]])
                    else:
                        S.op("dve", lambda e, half=half, hs=hs, tb=tb: e.tensor_tensor(out=acc[tb][:, hs], in0=acc[tb][:, hs], in1=pf[half][:],
                                                                                       op=ALU.add), reads=[r_pf, r_acc[tb]], writes=[r_acc[tb]])
        for tb in range(4):
            b = grp * 4 + tb
            S.op("dve", lambda e, tb=tb: e.tensor_tensor(out=y[:], in0=acc[tb][:], in1=G12[:, 1, :], op=ALU.mult),
                 reads=[r_acc[tb], r_g], writes=[r_y])
            S.op("dve", lambda e, tb=tb: e.scalar_tensor_tensor(out=y[:], in0=x1[tb][:], scalar=DN_ALPHA, in1=y[:], op0=ALU.mult, op1=ALU.add),
                 reads=[r_x1[tb], r_y], writes=[r_y])
            layer_norm_to(acc[tb], r_acc[tb], y, r_y, 1)
            S.dma("sp", xout_d[b * 128:(b + 1) * 128, :], acc[tb][:], reads=[r_acc[tb]])
        S.barrier()


def emit_phase2(K, io, debug_mix_d=None):
    nc, S, st = K.nc, K.S, K.st
    from contextlib import ExitStack
    A = lambda n, s, d: st.enter_context(nc.sbuf_tensor(n, s, d))
    T = Ctx()
    T.EA = A("T_EA", [128, 17, 4, 128], BF16)
    T.ECW = A("T_ECW", [128, 5, 8, 128], BF16)
    T.ES = A("T_ES", [128, 13, 8, 128], BF16)
    T.MB = A("T_MB", [128, 2, 128], BF16)
    T.OV = A("T_OV", [128, 8, 256], BF16)
    T.KC = A("T_KC", [64, 2, 1024], BF16)
    T.CV = A("T_CV", [128, 8, 2, 65], BF16)
    T.esink = A("T_esink", [128, 4], F32)
    T.modT = A("T_modT", [128, 48], F32)
    T.mod1 = A("T_mod1", [128, 48], F32)
    mixT = A("T_mixT", [128, 8, 512], BF16)
    T.r_tab, T.r_cmp, T.r_mod, r_mixT = Res("T_tab"), Res("T_cmp"), Res("T_mod"), Res("T_mixT")
    with ExitStack() as st2:
        s_mb = st2.enter_context(nc.sbuf_tensor("s_mb", [128, 2, 128], F32))
        s_ov = st2.enter_context(nc.sbuf_tensor("s_ov", [128, 8, 256], F32))
        r_s = Res("s_tmp")
        S.dma("sp", s_mb[:], io["mb"][:, :, :], writes=[r_s])
        S.dma("sp", s_ov[:], io["ov_t"][:, :, :], writes=[r_s])
        S.op("dve", lambda e: e.tensor_copy(out=T.MB[:], in_=s_mb[:]), reads=[r_s], writes=[T.r_tab])
        S.op("dve", lambda e: e.tensor_copy(out=T.OV[:], in_=s_ov[:]), reads=[r_s], writes=[T.r_tab])
        sk = io["sinks"]
        S.dma("sp", T.esink[:], bass.AP(tensor=sk.tensor, offset=sk.offset, ap=[[0, 128], [1, 4]]), writes=[T.r_tab])
        S.op("act", lambda e: e.activation(out=T.esink[:], in_=T.esink[:], func=AF.Exp), reads=[T.r_tab], writes=[T.r_tab])
        S.barrier()
    emit_mod(K, io["c"], io["ada_w"], io["ada_b"], T.modT, T.r_mod)
    S.op("dve", lambda e: e.tensor_scalar(out=T.mod1[:], in0=T.modT[:], scalar1=1.0, scalar2=None, op0=ALU.add),
         reads=[T.r_mod], writes=[T.r_mod])
    emit_tables(K, io["rel_bias"], io["bm"], io["cnt"], io["tscr"], T.EA, T.ECW, T.ES, T.r_tab)
    emit_compress(K, io["Gkt"], io["cmp_pos"], io["cmp_w1"], io["cmp_w2"], T.KC, T.CV, T.r_cmp)
    pc_d = {k: io[k] for k in ("am_abs", "vm_abs", "am_rel", "vm_rel", "ovrel", "cmpm")}
    Gkt3 = io["Gkt"]
    Gv3 = io["Gv"]
    for grp in range(LB // 4):
        with ExitStack() as st2:
            W = alloc_attn(K, st2)
            for tb in range(4):
                emit_attn_block(K, W, grp * 4 + tb, T, io["qT"], Gkt3, Gv3, io["Gktw"], io["Gvw"], io["gates"], pc_d, mixT, r_mixT, tb)
            if debug_mix_d is not None:
                S.dma("sp", debug_mix_d[grp, :, :], mixT[:].rearrange("p k q -> p (k q)"), reads=[r_mixT])
            S.barrier()
        emit_dense_group(K, T, grp, mixT, r_mixT, io["x"], io["xout"], io["w_out"], io["mlp_w1"], io["mlp_w2"], io["ln_g"], io["ln_b"])
    S.barrier()


P2_INPUTS = [("qT", [LB, NQH * 64, 128], BF16), ("Gkt", [NTL * NKH * 64, 128], BF16), ("Gv", [NTL * 128, NVH * 65], BF16),
             ("Gktw", [NTW * NKH * 64, 128], BF16), ("Gvw", [NTW * 128, NVH * 65], BF16),
             ("gates", [LT, 24], F32), ("x", [LT, D], F32), ("c", [1, D], F32), ("ada_w", [D, 6 * D], F32), ("ada_b", [1, 6 * D], F32),
             ("rel_bias", [32, 12], F32), ("bm", [32, DTOT], F32), ("cnt", [12, DTOT], F32), ("mb", [128, 2, 128], F32),
             ("ov_t", [128, 8, 256], F32), ("sinks", [1, 4], F32), ("cmp_pos", [2, 32, 64], F32), ("cmp_w1", [2, 2048, 256], F32),
             ("cmp_w2", [2, 256, 64], F32), ("w_out", [D, D], F32), ("mlp_w1", [D, 4 * D], F32), ("mlp_w2", [4 * D, D], F32),
             ("ln_g", [2, D], F32), ("ln_b", [2, D], F32), ("am_abs", [LB, 128, 256], F32), ("vm_abs", [LB, 128, 256], F32),
             ("am_rel", [LB, 128, NREL], F32), ("vm_rel", [LB, 128, NREL], F32), ("ovrel", [LB, 128, 3, NREL], F32),
             ("cmpm", [LB, 128, 2, 128], F32)]


def build_phase2(debug=False):
    from contextlib import ExitStack
    nc = bass.Bass("TRN2", target_bir_lowering=False)
    io = {}
    for name, shape, dt in P2_INPUTS:
        io[name] = nc.dram_tensor(name, shape, dt, kind="ExternalInput").ap()
    io["xout"] = nc.dram_tensor("xout", [LT, D], F32, kind="ExternalOutput").ap()
    io["tscr"] = nc.dram_tensor("tscr", [12, DTOT], F32).ap()
    dbg = nc.dram_tensor("dbg_mix", [LB // 4, 128, 8 * 512], BF16, kind="ExternalOutput").ap() if debug else None
    with ExitStack() as st:
        K = Ctx()
        K.nc, K.st, K.S = nc, st, Sched(nc, st)
        mk_consts(K)
        emit_phase2(K, io, dbg)
        K.ninst = K.S.ninst
    return nc


def make_p2_inputs(d, l, p1res, x):
    bm, cnt, mb, ov_t, ov = static_tables()
    Gkt = np.zeros((NTL, NKH * 64, 128), p1res[0]["ktp"].dtype)
    Gv = np.zeros((NTL, 128, NVH * 65), p1res[0]["vp"].dtype)
    for c in range(NCORES):
        Gkt[NPADL + c::NCORES] = np.asarray(p1res[c]["ktp"])[NPADT:]
        Gv[NPADL + c::NCORES] = np.asarray(p1res[c]["vp"])[NPADT:]
    ins = []
    for c in range(NCORES):
        ti = _tok_index(c)
        am_abs, vm_abs, am_rel, vm_rel, ovrel, cmpm = percore_tables(c, ov)
        ins.append({"qT": np.asarray(p1res[c]["qT"]), "Gkt": Gkt.reshape(NTL * NKH * 64, 128), "Gv": Gv.reshape(NTL * 128, NVH * 65),
                    "Gktw": np.ascontiguousarray(Gkt[c:c + NTW]).reshape(NTW * NKH * 64, 128),
                    "Gvw": np.ascontiguousarray(Gv[c:c + NTW]).reshape(NTW * 128, NVH * 65), "gates": np.asarray(p1res[c]["gates"]),
                    "x": np.ascontiguousarray(x[ti]), "c": d["c"], "ada_w": d["ada_w"][l], "ada_b": d["ada_b"][l][None],
                    "rel_bias": d["rel_bias"], "bm": bm, "cnt": cnt, "mb": mb, "ov_t": np.ascontiguousarray(ov_t),
                    "sinks": d["sinks"][l][None], "cmp_pos": d["cmp_pos"][l], "cmp_w1": d["cmp_w1"][l], "cmp_w2": d["cmp_w2"][l],
                    "w_out": d["w_out"][l], "mlp_w1": d["mlp_w1"][l], "mlp_w2": d["mlp_w2"][l], "ln_g": d["ln_g"][l],
                    "ln_b": d["ln_b"][l], "am_abs": am_abs, "vm_abs": vm_abs, "am_rel": am_rel, "vm_rel": vm_rel,
                    "ovrel": ovrel, "cmpm": cmpm})
    return ins


P1_INPUTS = [("c", [1, D], F32), ("ada_w", [D, 6 * D], F32), ("ada_b", [1, 6 * D], F32), ("x", [LT, D], F32),
             ("pos", [1, LT], I32), ("ropec", [128, 4], F32), ("w_in", [D, NWIN], F32)]


def build_phase1():
    from contextlib import ExitStack
    nc = bass.Bass("TRN2", target_bir_lowering=False)
    io = {}
    for name, shape, dt in P1_INPUTS:
        io[name] = nc.dram_tensor(name, shape, dt, kind="ExternalInput").ap()
    qT_d = nc.dram_tensor("qT", [LB, NQH * 64, 128], BF16, kind="ExternalOutput").ap()
    ktp_d = nc.dram_tensor("ktp", [RT, NKH * 64, 128], BF16, kind="ExternalOutput").ap()
    vp_d = nc.dram_tensor("vp", [RT, 128, NVH * 65], BF16, kind="ExternalOutput").ap()
    gates_d = nc.dram_tensor("gates", [LT, 24], F32, kind="ExternalOutput").ap()
    with ExitStack() as st:
        K = Ctx()
        K.nc, K.st, K.S = nc, st, Sched(nc, st)
        mk_consts(K)
        modT = st.enter_context(nc.sbuf_tensor("modT", [128, 48], F32))
        r = Res("modT")
        emit_mod(K, io["c"], io["ada_w"], io["ada_b"], modT, r)
        emit_phase1(K, io["x"], io["pos"], io["ropec"], io["w_in"], modT, r, qT_d, ktp_d, vp_d, gates_d)
        K.S.barrier()
    return nc


_PROGS = {}


def _prog(name):
    if name not in _PROGS:
        _PROGS[name] = build_phase1() if name == "p1" else build_phase2()
    return _PROGS[name]


def kernel(x, c, positions, w_in, w_out, rel_bias, sinks, cmp_pos, cmp_w1, cmp_w2, mlp_w1, mlp_w2, ada_w, ada_b, ln_g, ln_b):
    d = {"x": np.asarray(x), "c": np.asarray(c), "positions": np.asarray(positions), "w_in": np.asarray(w_in),
         "w_out": np.asarray(w_out), "rel_bias": np.asarray(rel_bias), "sinks": np.asarray(sinks),
         "cmp_pos": np.asarray(cmp_pos), "cmp_w1": np.asarray(cmp_w1), "cmp_w2": np.asarray(cmp_w2),
         "mlp_w1": np.asarray(mlp_w1), "mlp_w2": np.asarray(mlp_w2), "ada_w": np.asarray(ada_w),
         "ada_b": np.asarray(ada_b), "ln_g": np.asarray(ln_g), "ln_b": np.asarray(ln_b)}
    cores = list(range(NCORES))
    xcur = np.ascontiguousarray(d["x"][0])
    for l in range(DEPTH):
        r1 = run_bass_kernel_spmd(_prog("p1"), make_p1_inputs(d, l, xcur), core_ids=cores)
        r2 = run_bass_kernel_spmd(_prog("p2"), make_p2_inputs(d, l, r1.results, xcur), core_ids=cores)
        xn = np.empty_like(xcur)
        for cc in cores:
            xn[_tok_index(cc)] = np.asarray(r2.results[cc]["xout"])
        xcur = xn
    return xcur[None].astype(np.float32)
```
